# Optimizing a Trainium2 kernel written in Bass

```python
import jax, jax.numpy as jnp
from jax import lax
import numpy as np

D_MODEL = 2048
BATCH = 4
SEQ = 8192
DEPTH = 1

A_HEADS = 8
A_HEAD_DIM = 128
A_WIDTH = A_HEADS * A_HEAD_DIM
MOBA_BLOCK = 256
MOBA_TOPK = 3
MOBA_Q_CHUNK = 16
B_HEADS = 8
B_KEY_DIM = 128
B_VAL_DIM = 128
B_FWIDTH = B_HEADS * B_KEY_DIM
B_VWIDTH = B_HEADS * B_VAL_DIM
HGRN_CHUNK = 64
FFN_HIDDEN = -(-8 * D_MODEL // (3 * 256)) * 256
IN_SPLIT = (A_WIDTH, A_WIDTH, A_WIDTH, B_FWIDTH, B_FWIDTH, B_VWIDTH, B_VWIDTH, D_MODEL, D_MODEL)
IN_WIDTH = sum(IN_SPLIT)
LN_EPS = 1e-5
RMS_EPS = 1e-6
DEEPNORM_ALPHA = (2.0 * DEPTH) ** 0.25
DEEPNORM_BETA = (8.0 * DEPTH) ** -0.25

kernel_name = 'moba_hgrn2_gated_hybrid_deepnorm'


def layer_norm(x, g, b):
    xf = x.astype(jnp.float32)
    mu = jnp.mean(xf, axis=-1, keepdims=True)
    var = jnp.mean(jnp.square(xf - mu), axis=-1, keepdims=True)
    return ((xf - mu) * lax.rsqrt(var + LN_EPS) * g + b).astype(x.dtype)


def to_heads(t, n_heads):
    Bsz, T, W = t.shape
    return t.reshape(Bsz, T, n_heads, W // n_heads).transpose(0, 2, 1, 3)


def from_heads(t):
    Bsz, H, T, Dh = t.shape
    return t.transpose(0, 2, 1, 3).reshape(Bsz, T, H * Dh)


def moba_attention(q, k, v):
    Bsz, H, T, Dh = q.shape
    T_pad = -(-T // MOBA_BLOCK) * MOBA_BLOCK
    pad = ((0, 0), (0, 0), (0, T_pad - T), (0, 0))
    q = jnp.pad(q, pad)
    k = jnp.pad(k, pad)
    v = jnp.pad(v, pad)
    n_blk = T_pad // MOBA_BLOCK
    n_sel = min(MOBA_TOPK, n_blk)
    scale = Dh ** -0.5
    kb = k.reshape(Bsz, H, n_blk, MOBA_BLOCK, Dh)
    vb = v.reshape(Bsz, H, n_blk, MOBA_BLOCK, Dh)
    k_mean = jnp.mean(kb.astype(jnp.float32), axis=3)
    gate = jnp.einsum('bhtd,bhnd->bhtn', q.astype(jnp.float32), k_mean)
    q_blk = jnp.arange(T_pad) // MOBA_BLOCK
    fully_past = jnp.arange(n_blk)[None, :] < q_blk[:, None]
    gate = jnp.where(fully_past, gate, -jnp.inf)
    _, sel = lax.top_k(gate, n_sel)

    n_q = T_pad // MOBA_Q_CHUNK
    q_c = q.reshape(Bsz, H, n_q, MOBA_Q_CHUNK, Dh).transpose(2, 0, 1, 3, 4)
    sel_c = sel.reshape(Bsz, H, n_q, MOBA_Q_CHUNK, n_sel).transpose(2, 0, 1, 3, 4)
    starts = jnp.arange(n_q, dtype=jnp.int32) * MOBA_Q_CHUNK
    bi = jnp.arange(Bsz)[:, None, None, None]
    hi = jnp.arange(H)[None, :, None, None]

    def chunk_attn(args):
        qc, selc, start = args
        blk = start // MOBA_BLOCK
        blk_start = blk * MOBA_BLOCK
        k_sel = kb[bi, hi, selc]
        v_sel = vb[bi, hi, selc]
        s_sel = jnp.einsum('bhqd,bhqnjd->bhqnj', qc, k_sel).astype(jnp.float32) * scale
        slot_ok = jnp.arange(n_sel) < blk
        s_sel = jnp.where(slot_ok[:, None], s_sel, -jnp.inf)
        k_own = lax.dynamic_slice_in_dim(k, blk_start, MOBA_BLOCK, axis=2)
        v_own = lax.dynamic_slice_in_dim(v, blk_start, MOBA_BLOCK, axis=2)
        s_own = jnp.einsum('bhqd,bhjd->bhqj', qc, k_own).astype(jnp.float32) * scale
        q_pos = start + jnp.arange(MOBA_Q_CHUNK)
        k_pos = blk_start + jnp.arange(MOBA_BLOCK)
        s_own = jnp.where(k_pos[None, :] <= q_pos[:, None], s_own, -jnp.inf)
        scores = jnp.concatenate(
            [s_sel.reshape(Bsz, H, MOBA_Q_CHUNK, n_sel * MOBA_BLOCK), s_own], axis=-1)
        p = jax.nn.softmax(scores, axis=-1).astype(v.dtype)
        p_sel = p[..., :n_sel * MOBA_BLOCK].reshape(Bsz, H, MOBA_Q_CHUNK, n_sel, MOBA_BLOCK)
        p_own = p[..., n_sel * MOBA_BLOCK:]
        return (jnp.einsum('bhqnj,bhqnjd->bhqd', p_sel, v_sel)
                + jnp.einsum('bhqj,bhjd->bhqd', p_own, v_own))

    out = lax.map(chunk_attn, (q_c, sel_c, starts))
    out = out.transpose(1, 2, 0, 3, 4).reshape(Bsz, H, T_pad, Dh)
    return out[:, :, :T]


def hgrn2_recurrence(q, f_logit, i, lower_bound):
    Bsz, H, T, K = q.shape
    V = i.shape[-1]
    C = HGRN_CHUNK
    N = T // C
    lb = lower_bound[None, :, None, :]
    f = lb + (1.0 - lb) * jax.nn.sigmoid(f_logit.astype(jnp.float32))
    log_f = jnp.log(f)
    key = 1.0 - f

    def chunks(t):
        return t.astype(jnp.float32).reshape(Bsz, H, N, C, t.shape[-1]).transpose(2, 0, 1, 3, 4)

    causal = jnp.arange(C)[:, None] >= jnp.arange(C)[None, :]

    def step(S, inp):
        qc, kc, vc, gc = inp
        b = jnp.cumsum(gc, axis=2)
        o_inter = jnp.einsum('bhtk,bhkv->bhtv', qc * jnp.exp(b), S)
        rel = jnp.where(causal[None, None, :, :, None],
                        b[:, :, :, None, :] - b[:, :, None, :, :], -jnp.inf)
        A = jnp.einsum('bhtk,bhsk,bhtsk->bhts', qc, kc, jnp.exp(rel))
        o_intra = jnp.einsum('bhts,bhsv->bhtv', A, vc)
        b_last = b[:, :, -1:, :]
        S_new = (jnp.exp(b_last[:, :, 0, :])[..., None] * S
                 + jnp.einsum('bhsk,bhsv->bhkv', kc * jnp.exp(b_last - b), vc))
        return S_new, o_inter + o_intra

    S0 = jnp.zeros((Bsz, H, K, V), jnp.float32)
    _, outs = lax.scan(step, S0, (chunks(q), chunks(key), chunks(i), chunks(log_f)))
    o = outs.transpose(1, 2, 0, 3, 4).reshape(Bsz, H, T, V)
    return o.astype(i.dtype)


def token_mixer(h, w_in, w_proj_a, w_proj_b, w_out, hgrn_norm_g, lower_bound):
    z = h @ w_in
    parts = []
    off = 0
    for n in IN_SPLIT:
        parts.append(z[..., off:off + n])
        off += n
    qa, ka, va, qb, fb, ib, ogb, ga, gb = parts
    o_a = moba_attention(to_heads(qa, A_HEADS), to_heads(ka, A_HEADS), to_heads(va, A_HEADS))
    o_a = from_heads(o_a)
    o_b = hgrn2_recurrence(to_heads(jax.nn.silu(qb), B_HEADS), to_heads(fb, B_HEADS),
                           to_heads(ib, B_HEADS), lower_bound.reshape(B_HEADS, B_KEY_DIM))
    o_bf = o_b.astype(jnp.float32)
    o_bf = o_bf * lax.rsqrt(jnp.mean(jnp.square(o_bf), axis=-1, keepdims=True) + RMS_EPS)
    o_bf = o_bf * hgrn_norm_g.reshape(B_HEADS, 1, B_VAL_DIM).astype(jnp.float32)
    o_b = from_heads(o_bf.astype(h.dtype)) * jax.nn.sigmoid(ogb)
    y_a = o_a @ w_proj_a
    y_b = o_b @ w_proj_b
    merged = jax.nn.sigmoid(ga) * y_a + jax.nn.sigmoid(gb) * y_b
    return merged @ w_out


def swiglu_ffn(h, w_gate, w_up, w_down):
    return (jax.nn.silu(h @ w_gate) * (h @ w_up)) @ w_down


def setup_inputs(seed: int = 0) -> dict:
    key = jax.random.key(seed)
    ks = jax.random.split(key, 16)

    def nrm(k, shape, scale):
        return jax.random.normal(k, shape, jnp.float32) * scale

    return {
        'x': nrm(ks[0], (BATCH, SEQ, D_MODEL), 1.0),
        'w_in': nrm(ks[1], (DEPTH, D_MODEL, IN_WIDTH), D_MODEL ** -0.5),
        'w_proj_a': nrm(ks[2], (DEPTH, A_WIDTH, D_MODEL), A_WIDTH ** -0.5 * DEEPNORM_BETA),
        'w_proj_b': nrm(ks[3], (DEPTH, B_VWIDTH, D_MODEL), B_VWIDTH ** -0.5 * DEEPNORM_BETA),
        'w_out': nrm(ks[4], (DEPTH, D_MODEL, D_MODEL), D_MODEL ** -0.5 * DEEPNORM_BETA),
        'hgrn_norm_g': 1.0 + nrm(ks[5], (DEPTH, B_VWIDTH), 0.02),
        'hgrn_lb_logits': nrm(ks[6], (DEPTH + 1, B_FWIDTH), 0.5),
        'ln1_g': 1.0 + nrm(ks[7], (DEPTH, D_MODEL), 0.02),
        'ln1_b': nrm(ks[8], (DEPTH, D_MODEL), 0.02),
        'w_gate_ffn': nrm(ks[9], (DEPTH, D_MODEL, FFN_HIDDEN), D_MODEL ** -0.5),
        'w_up_ffn': nrm(ks[10], (DEPTH, D_MODEL, FFN_HIDDEN), D_MODEL ** -0.5),
        'w_down_ffn': nrm(ks[11], (DEPTH, FFN_HIDDEN, D_MODEL), FFN_HIDDEN ** -0.5 * DEEPNORM_BETA),
        'ln2_g': 1.0 + nrm(ks[12], (DEPTH, D_MODEL), 0.02),
        'ln2_b': nrm(ks[13], (DEPTH, D_MODEL), 0.02),
    }


def reference(x, w_in, w_proj_a, w_proj_b, w_out, hgrn_norm_g, hgrn_lb_logits,
              ln1_g, ln1_b, w_gate_ffn, w_up_ffn, w_down_ffn, ln2_g, ln2_b):
    lb_all = jnp.cumsum(jax.nn.softmax(hgrn_lb_logits.astype(jnp.float32), axis=0), axis=0)
    h = x
    for l in range(DEPTH):
        mix = token_mixer(h, w_in[l], w_proj_a[l], w_proj_b[l], w_out[l], hgrn_norm_g[l], lb_all[l])
        h = layer_norm(DEEPNORM_ALPHA * h + mix, ln1_g[l], ln1_b[l])
        ff = swiglu_ffn(h, w_gate_ffn[l], w_up_ffn[l], w_down_ffn[l])
        h = layer_norm(DEEPNORM_ALPHA * h + ff, ln2_g[l], ln2_b[l])
    return h
```

```python
import numpy as np
import concourse.bass as bass
import concourse.mybir as mybir
from concourse.bass_utils import run_bass_kernel_spmd
from contextlib import ExitStack

F32 = mybir.dt.float32
BF16 = mybir.dt.bfloat16
AF = mybir.ActivationFunctionType
ALU = mybir.AluOpType
AX = mybir.AxisListType

NCORES = 8
D = 2048
TT = 512
NT = 8
NCT = 8
H = 8
INW = 11264
FF = 5632
ALPHA = 2.0 ** 0.25
SCALE = 128.0 ** -0.5
NEG = -30000.0


class Buf:
    __slots__ = ("name", "w", "r", "dsem", "dcnt", "lo", "hi")

    def __init__(self, name):
        self.name = name
        self.w = None
        self.r = {}
        self.dsem = None
        self.dcnt = 0
        self.lo = None
        self.hi = None


class Eng:
    def __init__(self, name, sem):
        self.name = name
        self.sem = sem
        self.n = 0
        self.prog = []
        self.seen = {}


class KB:
    def __init__(self, nc, es):
        self.nc = nc
        self.es = es
        self.engs = {}
        for name in ("pe", "act", "dve", "pool", "sp"):
            sem = es.enter_context(nc.semaphore("s_" + name))
            self.engs[name] = Eng(name, sem)
        self.final = {}
        self.sbufs = []

    def buf(self, name):
        return Buf(name)

    def dmabuf(self, name):
        b = Buf(name)
        b.dsem = self.es.enter_context(self.nc.semaphore("d_" + name))
        return b

    def sb(self, name, shape, dtype, off, dma=False):
        t = self.nc.alloc_sbuf_tensor_at(name, list(shape), dtype, offset=off)
        n = 1
        for s in shape[1:]:
            n *= s
        nbytes = n * (2 if dtype == BF16 else 4)
        b = self.dmabuf(name) if dma else Buf(name)
        b.lo, b.hi = off, off + nbytes
        self.sbufs.append(b)
        return t, b

    def activate(self, bufs):
        for b in bufs:
            for o in self.sbufs:
                if o is b or o.lo is None or o.hi <= b.lo or o.lo >= b.hi:
                    continue
                evs = list(o.r.values())
                if o.w is not None:
                    evs.append(o.w)
                for ev in evs:
                    k = id(ev[0])
                    if k not in b.r or b.r[k][1] < ev[1]:
                        b.r[k] = ev

    def _waits(self, e, reads, writes):
        waits = {}

        def need(ev):
            if ev is None:
                return
            sem, val = ev
            if e.name == "pe" and sem is e.sem:
                return
            k = id(sem)
            if k not in waits or waits[k][1] < val:
                waits[k] = (sem, val)

        for b in reads:
            need(b.w)
        for b in writes:
            need(b.w)
            for ev in b.r.values():
                need(ev)
        for k, (sem, val) in waits.items():
            if e.seen.get(k, 0) < val:
                e.prog.append(("w", sem, val))
                e.seen[k] = val

    def _update(self, ev, reads, writes):
        k = id(ev[0])
        for b in reads:
            if k not in b.r or b.r[k][1] < ev[1]:
                b.r[k] = ev
        for b in writes:
            b.w = ev
            b.r = {}

    def op(self, engname, fn, reads=(), writes=(), inc=True):
        e = self.engs[engname]
        self._waits(e, reads, writes)
        if inc:
            e.n += 1
            ev = (e.sem, e.n)
            e.prog.append(("o", fn, e.sem, 1))
        else:
            ev = (e.sem, e.n + 1)
            e.prog.append(("o", fn, None, 0))
        self._update(ev, reads, writes)
        return ev

    def dma(self, qname, out, in_, reads=(), writes=(), sembuf=None, final=False):
        q = self.engs[qname]
        self._waits(q, reads, writes)
        sembuf.dcnt += 16
        ev = (sembuf.dsem, sembuf.dcnt)
        q.prog.append(("o", lambda eng, out=out, in_=in_: eng.dma_start(out=out, in_=in_), sembuf.dsem, 16))
        self._update(ev, reads, writes)
        if final:
            self.final[id(ev[0])] = ev
        return ev

    def emit(self):
        nc = self.nc
        sp = self.engs["sp"]
        for k, (sem, val) in self.final.items():
            sp.prog.append(("w", sem, val))

        def replay(e):
            def f(eng):
                for it in e.prog:
                    if it[0] == "w":
                        eng.wait_ge(it[1], it[2])
                    else:
                        ins = it[1](eng)
                        if it[2] is not None:
                            ins.then_inc(it[2], it[3])
            return f

        with nc.Block() as block:
            block.sync(replay(self.engs["sp"]))
            block.tensor(replay(self.engs["pe"]))
            block.scalar(replay(self.engs["act"]))
            block.vector(replay(self.engs["dve"]))
            block.gpsimd(replay(self.engs["pool"]))


class Rot:
    def __init__(self, items):
        self.items = items
        self.i = 0

    def next(self):
        it = self.items[self.i % len(self.items)]
        self.i += 1
        return it


def build_nc(n_own=NT, n_ctx=NCT):
    nc = bass.Bass("TRN2", target_bir_lowering=False)
    TOK = NT * TT

    def din(name, shape):
        return nc.dram_tensor(name, list(shape), F32, kind="ExternalInput").ap()

    def dscr(name, shape, dt=BF16):
        return nc.dram_tensor(name, list(shape), dt, kind="Internal").ap()

    xT_d = din("xT", [D, TOK])
    xcT_d = din("xcT", [D, TOK])
    x_d = din("x", [TOK, D])
    w_in_d = din("w_in", [D, INW])
    w_pa_d = din("w_pa", [1024, D])
    w_pb_d = din("w_pb", [1024, D])
    w_out_d = din("w_out", [D, D])
    w_g_d = din("w_g", [D, FF])
    w_u_d = din("w_u", [D, FF])
    w_d_d = din("w_d", [FF, D])
    lbl_d = din("lbl", [128, 16])
    gn_d = din("gn", [128, 8])
    lnp_d = din("lnp", [128, 4 * D])
    blkb_d = din("blkb", [128, 32])
    y_d = nc.dram_tensor("y", [TOK, D], F32, kind="ExternalOutput").ap()

    wi_s = dscr("wi_s", [D, INW])
    wpa_s = dscr("wpa_s", [1024, D])
    wpb_s = dscr("wpb_s", [1024, D])
    wo_s = dscr("wo_s", [D, D])
    wg_s = dscr("wg_s", [D, FF])
    wu_s = dscr("wu_s", [D, FF])
    wd_s = dscr("wd_s", [FF, D])
    kT_s = dscr("kT_s", [H, 128, 2 * TOK])
    v_s = dscr("v_s", [H, 128, 2 * TOK // 128, 130])

    with ExitStack() as es:
        kb = KB(nc, es)
        op = kb.op

        base = 16512
        cur = [base]

        def alloc(name, shape, dtype, dma=False):
            n = 1
            for s in shape[1:]:
                n *= s
            nbytes = n * (2 if dtype == BF16 else 4)
            nbytes = (nbytes + 31) // 32 * 32
            t, b = kb.sb(name, shape, dtype, cur[0], dma=dma)
            cur[0] += nbytes
            return t, b

        ident, ID = alloc("ident", [128, 128], BF16)
        identf, IDF = alloc("identf", [128, 128], F32)
        ind, IND = alloc("ind", [128, 32, 128], BF16)
        causb, CAUS = alloc("causb", [128, 2, 256], BF16)
        mblk, MBLK = alloc("mblk", [128, 128], F32)
        rowm, ROWM = alloc("rowm", [128, 2], F32)
        lnp, LNP = alloc("lnp", [128, 4 * D], F32, dma=True)
        blkb, BLKB = alloc("blkb", [128, 32], F32, dma=True)
        lbl, LBL = alloc("lbl", [128, 16], F32, dma=True)
        gn, GN = alloc("gn", [128, 8], F32, dma=True)
        lb, LB = alloc("lb", [128, 8], F32)
        oml, OML = alloc("oml", [128, 8], F32)
        S, SB_ = alloc("S", [128, H, 128], F32)
        ksT, KST = alloc("ksT", [128, H, 32], BF16)
        onesb, ONES = alloc("onesb", [128, 128], BF16)
        tmpi, TMPI = alloc("tmpi", [128, 256], F32)
        onesf, ONESF = alloc("onesf", [128, TT], F32)
        wslots = []
        for i in range(5):
            wslots.append(alloc("w%d" % i, [128, 4096], BF16, dma=True))
        wrot = Rot(wslots)
        xT, XT = alloc("xTt", [128, 16, TT], BF16, dma=True)
        OV0 = cur[0]
        xT2, XT2 = kb.sb("xTt2", [128, 16, TT], BF16, OV0 + 78 * 1024, dma=True)
        XCUR = [xT, XT]
        assert OV0 + 97 * 1024 + 512 <= 229344, OV0

        def ov(name, shape, dtype, koff, dma=False):
            return kb.sb(name, shape, dtype, OV0 + koff, dma=dma)

        K1 = 1024
        qT, QT = ov("qT", [128, H, TT], BF16, 32 * K1)
        kT, KT = ov("kT", [128, H, TT], BF16, 40 * K1, dma=True)
        vaug, VAUG = ov("vaug", [128, H, 4, 130], BF16, 48 * K1, dma=True)
        hsl = []
        for i in range(4):
            kh, KH = ov("kh%d" % i, [128, 512], BF16, 57 * K1 + i * 2112, dma=True)
            vh = nc.alloc_sbuf_tensor_at("vh%d" % i, [128, 4, 130], BF16, offset=OV0 + 57 * K1 + i * 2112 + 1024)
            KH.hi = OV0 + 57 * K1 + (i + 1) * 2112
            hsl.append((kh, vh, KH))
        hrot = Rot(hsl)
        pts = [ov("pt%d" % i, [128, 512], BF16, 66 * K1 + i * 1024) for i in range(4)]
        ptrot = Rot(pts)
        biasT, BIAST = ov("biasT", [128, H, TT], BF16, 70 * K1)
        oatm, OATM = ov("oatm", [128, 4, H, 128], BF16, 78 * K1)
        rsb = [ov("rs%d" % i, [128, TT], F32, 78 * K1 + i * 2048) for i in range(2)]
        accs = [ov("acc%d" % i, [128, TT], BF16, 82 * K1 + i * 1024) for i in range(4)]
        gsb, GSB = ov("gsb", [128, H, 32], F32, 86 * K1)
        t8, T8 = ov("t8", [128, H, 8], F32, 87 * K1)
        thr, THR = ov("thr", [128, H], F32, 87 * K1 + 256)
        m01, M01 = ov("m01", [128, H, 32], BF16, 87 * K1 + 320)
        vb, VB = ov("vb", [128, 2, 32], F32, 65 * K1 + 256)
        rec, REC = ov("rec", [128, 8], F32, 65 * K1 + 512)
        oaT, OAT = ov("oaT", [128, H, TT], BF16, 88 * K1)
        obT, OBT = ov("obT", [128, H, TT], BF16, 80 * K1)
        P1B = [QT, KT, VAUG, BIAST, GSB, T8, THR, M01, VB, OAT] + [h[2] for h in hsl] + [p[1] for p in pts] + [r[1] for r in rsb] + [a[1] for a in accs]
        qdT, QDT = ov("qdT", [128, H, TT], BF16, 0)
        kdT, KDT = ov("kdT", [128, H, TT], BF16, 8 * K1)
        kdcs = [ov("kdc%d" % i, [128, 2, H, 128], BF16, 16 * K1 + i * 4096) for i in range(2)]
        vv, VV = ov("vv", [128, 4, 1024], BF16, 32 * K1)
        gsig, GSIG = ov("gsig", [128, H, TT], BF16, 40 * K1)
        tmA = [ov("tmA%d" % i, [128, TT], F32, 48 * K1 + i * 2048) for i in range(5)]
        tmB = [ov("tmB%d" % i, [128, TT], F32, 24 * K1 + i * 2048) for i in range(4)] + [ov("tmB4", [128, TT], F32, 75 * K1)]
        tm = tmA
        atm, ATM = ov("atm", [128, H, 128], BF16, 60 * K1)
        smb, SMB = ov("smb", [128, H, 128], BF16, 62 * K1)
        tts, TTS = ov("tts", [128, H, 128], F32, 64 * K1)
        sq, SQ = ov("sq", [128, H, 128], BF16, 68 * K1)
        rstd, RSTD = ov("rstd", [128, H, 128], F32, 70 * K1)
        em_c, EMC = ov("em_c", [128, H, 8], F32, 74 * K1)
        el_c, ELC = ov("el_c", [128, H, 8], F32, 74 * K1 + 256)
        elb_c, ELBC = ov("elb_c", [128, H, 8], F32, 74 * K1 + 512)
        prvs = [ov("prv_c%d" % i, [128, 8], F32, 74 * K1 + 768 + i * 32) for i in range(2)]
        P2B = ([QDT, KDT, VV, GSIG, ATM, SMB, TTS, SQ, RSTD, EMC, ELC, ELBC, OBT] + [t[1] for t in tmA + tmB]
               + [k[1] for k in kdcs] + [p[1] for p in prvs])
        mrgT, MRGT = ov("mrgT", [128, 16, TT], BF16, 48 * K1)
        p3t = [ov("p3t%d" % i, [128, TT], F32, 64 * K1 + i * 2048) for i in range(4)]
        rr, RR = ov("rr", [128, 4, D], F32, 0, dma=True)
        h1T, H1T = ov("h1T", [128, 16, TT], BF16, 32 * K1)
        actT, ACTT = ov("actT", [128, 44, TT], BF16, 48 * K1)
        sgt = [ov("sgt%d" % i, [128, TT], F32, 92 * K1 + i * 2048) for i in range(2)]
        st, ST = ov("st", [128, 24], F32, 96 * K1)
        mv, MV = ov("mv", [128, 4], F32, 96 * K1 + 128)
        P3A = [MRGT] + [t[1] for t in p3t]
        P3B = [RR, H1T, ST, MV]
        P3C = [ACTT] + [t[1] for t in sgt]

        banks = []
        for i in range(8):
            t = es.enter_context(nc.psum_tensor("pb%d" % i, [128, 512], F32))
            banks.append((t, Buf("pb%d" % i)))

        WI, WPA, WPB, WO, WG, WU, WD = [kb.dmabuf(n) for n in ("cwi", "cwpa", "cwpb", "cwo", "cwg", "cwu", "cwd")]
        KSK = [Buf("ksk%d" % i) for i in range(16)]
        KSV = [Buf("ksv%d" % i) for i in range(16)]
        YB = Buf("y")
        WIA = kb.dmabuf("cwia")
        WIK = kb.dmabuf("cwik")
        WIV = kb.dmabuf("cwiv")
        WII = kb.dmabuf("cwii")
        RRST = kb.dmabuf("rrst")

        def mm(out, lhsT, rhs, start, stop, reads, writes, inc=None):
            if inc is None:
                inc = stop
            op("pe", lambda e, out=out, lhsT=lhsT, rhs=rhs, start=start, stop=stop:
               e.matmul(out, lhsT=lhsT, rhs=rhs, start=start, stop=stop, skip_group_check=True),
               reads=reads, writes=writes, inc=inc)

        def tp(out, in_, idn, reads, writes):
            op("pe", lambda e, out=out, in_=in_, idn=idn: e.transpose(out, in_, idn), reads=reads, writes=writes)

        def act(out, in_, func, reads, writes, scale=1.0, bias=0.0):
            op("act", lambda e, out=out, in_=in_, func=func, scale=scale, bias=bias:
               e.activation(out=out, in_=in_, func=func, bias=bias, scale=scale), reads=reads, writes=writes)

        def tt(eng, out, in0, in1, o, reads, writes):
            op(eng, lambda e, out=out, in0=in0, in1=in1, o=o: e.tensor_tensor(out=out, in0=in0, in1=in1, op=o),
               reads=reads, writes=writes)

        def ts(eng, out, in0, s1, s2, o0, o1, reads, writes):
            if o1 is None:
                op(eng, lambda e, out=out, in0=in0, s1=s1, o0=o0:
                   e.tensor_scalar(out=out, in0=in0, scalar1=s1, scalar2=None, op0=o0), reads=reads, writes=writes)
            else:
                op(eng, lambda e, out=out, in0=in0, s1=s1, s2=s2, o0=o0, o1=o1:
                   e.tensor_scalar(out=out, in0=in0, scalar1=s1, scalar2=s2, op0=o0, op1=o1), reads=reads, writes=writes)

        def stt(out, in0, sc, in1, o0, o1, reads, writes):
            op("dve", lambda e, out=out, in0=in0, sc=sc, in1=in1, o0=o0, o1=o1:
               e.scalar_tensor_tensor(out=out, in0=in0, scalar=sc, in1=in1, op0=o0, op1=o1), reads=reads, writes=writes)

        def cp(eng, out, in_, reads, writes):
            if eng == "act":
                act(out, in_, AF.Copy, reads, writes)
            else:
                op(eng, lambda e, out=out, in_=in_: e.tensor_copy(out=out, in_=in_), reads=reads, writes=writes)

        def mset(eng, ap, val, writes):
            op(eng, lambda e, ap=ap, val=val: e.memset(ap, val), writes=writes)

        def wtile(src_ap, kc, ncols, src_bufs):
            t, b = wrot.next()
            view = t[:, 0:kc * ncols].rearrange("p (c n) -> p c n", n=ncols)
            kb.dma("sp", view, src_ap, reads=src_bufs, writes=[b], sembuf=b)
            return view, b

        def wview(scr, r0, kc, c0, ncols):
            return scr.rearrange("(c p) n -> p c n", p=128)[:, r0:r0 + kc, c0:c0 + ncols]

        kb.dma("sp", lnp[:], lnp_d, writes=[LNP], sembuf=LNP)
        kb.dma("sp", blkb[:], blkb_d, writes=[BLKB], sembuf=BLKB)
        kb.dma("sp", lbl[:], lbl_d, writes=[LBL], sembuf=LBL)
        kb.dma("sp", gn[:], gn_d, writes=[GN], sembuf=GN)
        mset("pool", identf[:], 1.0, [IDF])
        op("pool", lambda e: e.affine_select(out=identf[:], in_=identf[:], pattern=[[-1, 128]], compare_op=ALU.is_equal,
                                             fill=0.0, base=0, channel_multiplier=1), reads=[IDF], writes=[IDF])
        cp("dve", ident[:], identf[:], [IDF], [ID])
        mset("dve", onesb[:], 1.0, [ONES])
        mset("dve", onesf[:], 1.0, [ONESF])
        mset("pool", ind[:], NEG, [IND])
        op("pool", lambda e: e.affine_select(out=ind[:], in_=ind[:], pattern=[[-1, 32], [0, 128]], compare_op=ALU.is_equal,
                                             fill=0.0, base=0, channel_multiplier=1), reads=[IND], writes=[IND])
        mset("pool", causb[:], NEG, [CAUS])
        op("pool", lambda e: e.affine_select(out=causb[:], in_=causb[:], pattern=[[128, 2], [-1, 256]], compare_op=ALU.is_gt,
                                             fill=0.0, base=0, channel_multiplier=1), reads=[CAUS], writes=[CAUS])
        mset("pool", mblk[:], 1.0, [MBLK])
        op("pool", lambda e: e.affine_select(out=mblk[:], in_=mblk[:], pattern=[[1, 128]], compare_op=ALU.is_ge,
                                             fill=0.0, base=0, channel_multiplier=-1), reads=[MBLK], writes=[MBLK])
        op("pool", lambda e: e.affine_select(out=mblk[:, 64:128], in_=mblk[:, 64:128], pattern=[[0, 64]], compare_op=ALU.is_ge,
                                             fill=0.0, base=-64, channel_multiplier=1), reads=[MBLK], writes=[MBLK])
        mset("pool", rowm[:], 1.0, [ROWM])
        op("pool", lambda e: e.affine_select(out=rowm[:, 0:1], in_=rowm[:, 0:1], pattern=[[0, 1]], compare_op=ALU.is_gt,
                                             fill=0.0, base=64, channel_multiplier=-1), reads=[ROWM], writes=[ROWM])
        op("pool", lambda e: e.affine_select(out=rowm[:, 1:2], in_=rowm[:, 1:2], pattern=[[0, 1]], compare_op=ALU.is_ge,
                                             fill=0.0, base=-64, channel_multiplier=1), reads=[ROWM], writes=[ROWM])
        tt("dve", tmpi[:, 0:8], lbl[:, 0:8], lbl[:, 8:16], ALU.subtract, [LBL], [TMPI])
        act(lb[:], tmpi[:, 0:8], AF.Sigmoid, [TMPI], [LB])
        act(oml[:], tmpi[:, 0:8], AF.Sigmoid, [TMPI], [OML], scale=-1.0)
        mset("dve", S[:], 0.0, [SB_])
        mset("dve", ksT[:], 0.0, [KST])

        def conv(dst, src, rows, sem, c0=None, c1=None):
            for r0 in range(0, rows, 128):
                if c0 is None:
                    kb.dma("pool", dst[r0:r0 + 128, :], src[r0:r0 + 128, :], writes=[sem], sembuf=sem)
                else:
                    kb.dma("pool", dst[r0:r0 + 128, c0:c1], src[r0:r0 + 128, c0:c1], writes=[sem], sembuf=sem)

        def load_x(src, tok0, alt=False):
            xt_, XT_ = (xT2, XT2) if alt else (xT, XT)
            kb.dma("pool", xt_[:], src.rearrange("(c p) t -> p c t", p=128)[:, :, tok0:tok0 + TT], writes=[XT_], sembuf=XT_)

        bankrot = Rot(banks[0:4])

        def proj_fm(scr, scr_buf, col0, ncols, kc, rhs_fn, rhs_bufs, consume, brot=None):
            br = brot or bankrot
            for pr in range(ncols // 256):
                wv, wb = wtile(wview(scr, 0, kc, col0 + pr * 256, 256), kc, 256, [scr_buf])
                for j in range(2):
                    bt, bb = br.next()
                    for c in range(kc):
                        mm(bt[:, :], wv[:, c, j * 128:(j + 1) * 128], rhs_fn(c), c == 0, c == kc - 1,
                           [wb] + rhs_bufs, [bb])
                    consume(pr * 2 + j, bt, bb)

        def proj_tm(scr, scr_buf, col0, kc, lhs_fn, lhs_bufs, consume, bset):
            pieces = []
            r = 0
            while r < kc:
                n = min(8, kc - r)
                pieces.append((r, n))
                r += n
            for (r0, n) in pieces:
                wv, wb = wtile(wview(scr, r0, n, col0, 512), n, 512, [scr_buf])
                for sub in range(4):
                    bt, bb = bset[sub]
                    for ci in range(n):
                        c = r0 + ci
                        mm(bt[:, :], lhs_fn(c, sub), wv[:, ci, :], c == 0, c == kc - 1, [wb] + lhs_bufs, [bb],
                           inc=(c == kc - 1) or (sub == 3 and ci == n - 1))
            for sub in range(4):
                consume(sub, bset[sub][0], bset[sub][1])

        xrhs = lambda c: XCUR[0][:, c, :]
        xlhs = lambda c, sub: XCUR[0][:, c, sub * 128:(sub + 1) * 128]

        def kv_proj(tile_idx):
            def k_cons(j, bt, bb):
                cp("act" if j % 2 else "dve", kT[:, j, :], bt[:, :], [bb], [KT])
            proj_fm(wi_s, WIK, 1024, 1024, 16, xrhs, [XCUR[1]], k_cons)
            op("dve", lambda e: e.tensor_reduce(out=tmpi[:, 0:16], in_=kT[:].rearrange("p h (b k) -> p (h b) k", k=256),
                                                axis=AX.X, op=ALU.add), reads=[KT], writes=[TMPI])
            cp("dve", ksT[:, :, 2 * tile_idx:2 * tile_idx + 2], tmpi[:, 0:16].rearrange("p (h b) -> p h b", b=2),
               [TMPI], [KST])
            kb.dma("pool", kT_s[:, :, tile_idx * TT:(tile_idx + 1) * TT].rearrange("h p t -> p h t"), kT[:],
                   reads=[KT], writes=[KSK[tile_idx]], sembuf=KT)
            mset("dve", vaug[:, :, :, 128:130], 1.0, [VAUG])
            for ct in range(2):
                def v_cons(sub, bt, bb, ct=ct):
                    cp("act" if sub % 2 else "dve", vaug[:, 4 * ct:4 * ct + 4, sub, 0:128],
                       bt[:, :].rearrange("p (h d) -> p h d", d=128), [bb], [VAUG])
                proj_tm(wi_s, WIV, 2048 + ct * 512, 16, xlhs, [XCUR[1]], v_cons, banks[4:8])
            kb.dma("pool", v_s[:, :, tile_idx * 4:(tile_idx + 1) * 4, :].rearrange("h p s c -> p h s c"), vaug[:],
                   reads=[VAUG], writes=[KSV[tile_idx]], sembuf=VAUG)

        def hgrn_ctx():
            kb.activate(P2B)
            for ct in range(2):
                def i_cons(sub, bt, bb, ct=ct):
                    cp("act" if sub % 2 else "dve", vv[:, sub, ct * 512:(ct + 1) * 512], bt[:, :], [bb], [VV])
                proj_tm(wi_s, WII, 5120 + ct * 512, 16, xlhs, [XCUR[1]], i_cons, banks[4:8])

            def f_cons(h, bt, bb):
                (sg, SG), (sn, SN), (gg, GG), (ep, EP), (em, EM) = tmA if h % 2 == 0 else tmB
                act(sg[:], bt[:, :], AF.Sigmoid, [bb], [SG])
                act(sn[:], bt[:, :], AF.Sigmoid, [bb], [SN], scale=-1.0)
                act(gg[:], sg[:], AF.Ln, [SG, OML, LB], [GG], scale=oml[:, h:h + 1], bias=lb[:, h:h + 1])
                op("dve", lambda e: e.tensor_tensor_scan(out=sg[:], data0=onesf[:], data1=gg[:], initial=0.0,
                                                         op0=ALU.mult, op1=ALU.add), reads=[ONESF, GG], writes=[SG])
                act(ep[:], sg[:], AF.Exp, [SG], [EP], scale=-1.0, bias=sg[:, TT - 1:TT])
                stt(kdT[:, h, :], sn[:], oml[:, h:h + 1], ep[:], ALU.mult, ALU.mult, [SN, OML, EP], [KDT])
                act(elb_c[:, h, 0:1], sg[:, TT - 1:TT], AF.Exp, [SG], [ELBC])
            proj_fm(wi_s, WIA, 4096, 1024, 16, xrhs, [XCUR[1]], f_cons)
            bU = banks[0:2]
            trb = Rot(banks[2:4])
            for sub in range(4):
                ssl = slice(sub * 128, (sub + 1) * 128)
                bT_, BT_ = trb.next()
                btb = bT_[:, :].bitcast(BF16)
                for h in range(H):
                    tp(btb[:, h * 128:(h + 1) * 128], kdT[:, h, ssl], ident[:], [KDT, ID], [BT_])
                kdc, KDC = kdcs[sub % 2]
                cp("act" if sub % 2 else "dve", kdc[:, 0, :, :], btb.rearrange("p (h k) -> p h k", k=128), [BT_], [KDC])
                for h in range(H):
                    bt, bb = bU[h // 4]
                    mm(bt[:, (h % 4) * 128:(h % 4 + 1) * 128], kdc[:, 0, h, :], vv[:, sub, h * 128:(h + 1) * 128],
                       sub == 0 and h % 4 == 0, sub == 3, [KDC, VV], [bb], inc=(h % 4 == 3))
            tt("dve", tts[:], S[:], elb_c[:, :, 0:1].to_broadcast([128, H, 128]), ALU.mult, [SB_, ELBC], [TTS])
            for g in range(2):
                bt, bb = bU[g]
                tt("dve", S[:, 4 * g:4 * g + 4, :], bt[:, :].rearrange("p (h v) -> p h v", v=128),
                   tts[:, 4 * g:4 * g + 4, :], ALU.add, [bb, TTS], [SB_])

        def hgrn(with_out):
            kb.activate(P2B)
            T_SIG = tmA[0]
            if with_out:
                def q_cons(j, bt, bb):
                    act(qdT[:, j, :], bt[:, :], AF.Silu, [bb], [QDT])
                run_jobs(8)
            for ct in range(2):
                def i_cons(sub, bt, bb, ct=ct):
                    cp("act" if sub % 2 else "dve", vv[:, sub, ct * 512:(ct + 1) * 512], bt[:, :], [bb], [VV])
                proj_tm(wi_s, WII, 5120 + ct * 512, 16, xlhs, [XCUR[1]], i_cons, banks[4:8])

            def f_cons(h, bt, bb):
                (sg, SG), (sn, SN), (gg, GG), (ep, EP), (em, EM) = tmA if h % 2 == 0 else tmB
                prv_c, PRVC = prvs[h % 2]
                act(sg[:], bt[:, :], AF.Sigmoid, [bb], [SG])
                act(sn[:], bt[:, :], AF.Sigmoid, [bb], [SN], scale=-1.0)
                act(gg[:], sg[:], AF.Ln, [SG, OML, LB], [GG], scale=oml[:, h:h + 1], bias=lb[:, h:h + 1])
                op("dve", lambda e: e.tensor_tensor_scan(out=sg[:], data0=onesf[:], data1=gg[:], initial=0.0,
                                                         op0=ALU.mult, op1=ALU.add), reads=[ONESF, GG], writes=[SG])
                Bc = sg[:].rearrange("p (c t) -> p c t", t=64)
                tt("dve", gg[:].rearrange("p (c t) -> p c t", t=64), Bc, Bc[:, :, 31:32].to_broadcast([128, 8, 64]),
                   ALU.subtract, [SG], [GG])
                act(ep[:], gg[:], AF.Exp, [GG], [EP])
                act(em[:], gg[:], AF.Exp, [GG], [EM], scale=-1.0)
                stt(kdT[:, h, :], sn[:], oml[:, h:h + 1], em[:], ALU.mult, ALU.mult, [SN, OML, EM], [KDT])
                if with_out:
                    tt("dve", qdT[:, h, :], qdT[:, h, :], ep[:], ALU.mult, [QDT, EP], [QDT])
                mset("dve", prv_c[:, 0:1], 0.0, [PRVC])
                cp("dve", prv_c[:, 1:8], Bc[:, 0:7, 63], [SG], [PRVC])
                tt("dve", prv_c[:, :], Bc[:, :, 31], prv_c[:, :], ALU.subtract, [SG, PRVC], [PRVC])
                act(em_c[:, h, :], prv_c[:, :], AF.Exp, [PRVC], [EMC])
                cp("dve", el_c[:, h, :], ep[:].rearrange("p (c t) -> p c t", t=64)[:, :, 63], [EP], [ELC])
                tt("dve", elb_c[:, h, :], em_c[:, h, :], el_c[:, h, :], ALU.mult, [EMC, ELC], [ELBC])
            if with_out:
                for pr in range(4):
                    wq, WQb = wtile(wview(wi_s, 0, 16, 3072 + pr * 256, 256), 16, 256, [WI])
                    wf, WFb = wtile(wview(wi_s, 0, 16, 4096 + pr * 256, 256), 16, 256, [WIA])
                    for j in range(2):
                        hh = pr * 2 + j
                        js = slice(j * 128, (j + 1) * 128)
                        bt, bb = bankrot.next()
                        for c in range(16):
                            mm(bt[:, :], wq[:, c, js], xrhs(c), c == 0, c == 15, [WQb, XCUR[1]], [bb])
                        q_cons(hh, bt, bb)
                        bt, bb = bankrot.next()
                        for c in range(16):
                            mm(bt[:, :], wf[:, c, js], xrhs(c), c == 0, c == 15, [WFb, XCUR[1]], [bb])
                        f_cons(hh, bt, bb)
            else:
                proj_fm(wi_s, WIA, 4096, 1024, 16, xrhs, [XCUR[1]], f_cons)

            bA = banks[0:2]
            bO = banks[2:4]
            bU = banks[4:6]
            bT_, BT_ = banks[6]
            bS = banks[6:8]
            ogw = [None]

            def og_task(j):
                if j % 2 == 0:
                    ogw[0] = wtile(wview(wi_s, 0, 16, 6144 + (j // 2) * 256, 256), 16, 256, [WI])
                wv, wb = ogw[0]
                bt, bb = banks[7]
                for c in range(16):
                    mm(bt[:, :], wv[:, c, (j % 2) * 128:(j % 2 + 1) * 128], xrhs(c), c == 0, c == 15, [wb, XCUR[1]], [bb])
                tg, TG = (tmA if j % 2 == 0 else tmB)[0]
                act(tg[:], bt[:, :], AF.Sigmoid, [bb], [TG])
                ts("pool", gsig[:, j, :], tg[:], gn[:, j:j + 1], 1.0, ALU.mult, ALU.mult, [TG, GN], [GSIG])
            for sub in range(4):
                ssl = slice(sub * 128, (sub + 1) * 128)
                btb = bT_[:, :].bitcast(BF16)
                for h in range(H):
                    tp(btb[:, h * 128:(h + 1) * 128], kdT[:, h, ssl], ident[:], [KDT, ID], [BT_])
                kdc, KDC = kdcs[sub % 2]
                for c in range(2):
                    if c == 0:
                        ts("dve", kdc[:, c, :, :], btb.rearrange("p (h k) -> p h k", k=128), rowm[:, c:c + 1], None,
                           ALU.mult, None, [BT_, ROWM], [KDC])
                    else:
                        act(kdc[:, c, :, :], btb.rearrange("p (h k) -> p h k", k=128), AF.Copy, [BT_, ROWM], [KDC],
                            scale=rowm[:, c:c + 1])
                if with_out:
                    for h in range(H):
                        bt, bb = bA[h // 4]
                        mm(bt[:, (h % 4) * 128:(h % 4 + 1) * 128], kdT[:, h, ssl], qdT[:, h, ssl], True, True, [KDT, QDT], [bb])
                    for g in range(2):
                        bt, bb = bA[g]
                        tt("dve", atm[:, 4 * g:4 * g + 4, :], bt[:, :].rearrange("p (h t) -> p h t", t=128),
                           mblk[:, :].unsqueeze(1).to_broadcast([128, 4, 128]), ALU.mult, [bb, MBLK], [ATM])
                    for h in range(H):
                        bt, bb = bO[h // 4]
                        mm(bt[:, (h % 4) * 128:(h % 4 + 1) * 128], vv[:, sub, h * 128:(h + 1) * 128], atm[:, h, :],
                           h % 4 == 0, False, [VV, ATM], [bb], inc=(h == H - 1))
                for c in range(2):
                    cc = sub * 2 + c
                    if with_out:
                        tt("dve", smb[:], S[:], em_c[:, :, cc:cc + 1].to_broadcast([128, H, 128]), ALU.mult, [SB_, EMC], [SMB])
                        for h in range(H):
                            bt, bb = bO[h // 4]
                            o0 = (h % 4) * 128 + c * 64
                            mm(bt[:, o0:o0 + 64], smb[:, h, :], qdT[:, h, sub * 128 + c * 64: sub * 128 + (c + 1) * 64],
                               False, c == 1, [SMB, QDT], [bb], inc=(h == H - 1))
                    for h in range(H):
                        bt, bb = bU[h // 4]
                        mm(bt[:, (h % 4) * 128:(h % 4 + 1) * 128], kdc[:, c, h, :], vv[:, sub, h * 128:(h + 1) * 128],
                           True, True, [KDC, VV], [bb], inc=(h % 4 == 3))
                    tt("dve", tts[:], S[:], elb_c[:, :, cc:cc + 1].to_broadcast([128, H, 128]), ALU.mult, [SB_, ELBC], [TTS])
                    for g in range(2):
                        bt, bb = bU[g]
                        tt("dve", S[:, 4 * g:4 * g + 4, :], bt[:, :].rearrange("p (h v) -> p h v", v=128),
                           el_c[:, 4 * g:4 * g + 4, cc:cc + 1].to_broadcast([128, 4, 128]), ALU.mult, [bb, ELC], [SB_])
                    tt("dve", S[:], S[:], tts[:], ALU.add, [SB_, TTS], [SB_])
                    if with_out:
                        og_task(cc)
                if with_out:
                    for g in range(2):
                        bt, bb = bO[g]
                        act(sq[:, 4 * g:4 * g + 4, :], bt[:, :].rearrange("p (h t) -> p h t", t=128), AF.Square, [bb], [SQ])
                    for h in range(H):
                        bt, bb = bS[h // 4]
                        mm(bt[:, (h % 4) * 128:(h % 4 + 1) * 128], onesb[:], sq[:, h, :], True, True, [ONES, SQ], [bb],
                           inc=(h % 4 == 3))
                    for g in range(2):
                        bt, bb = bS[g]
                        act(rstd[:, 4 * g:4 * g + 4, :], bt[:, :].rearrange("p (h t) -> p h t", t=128), AF.Ln, [bb], [RSTD],
                            scale=1.0 / 128.0, bias=1e-6)
                    act(rstd[:], rstd[:], AF.Exp, [RSTD], [RSTD], scale=-0.5)
                    for g in range(2):
                        bt, bb = bO[g]
                        tt("dve", obT[:, 4 * g:4 * g + 4, ssl], bt[:, :].rearrange("p (h t) -> p h t", t=128),
                           rstd[:, 4 * g:4 * g + 4, :], ALU.mult, [bb, RSTD], [OBT])
            if with_out:
                tt("dve", obT[:], obT[:], gsig[:], ALU.mult, [OBT, GSIG], [OBT])

        def attention(t):
            bG, BG = banks[6]
            bTt, BTt = banks[7]
            for hf in range(2):
                cp("dve", vb[:, hf, :], blkb[:], [BLKB], [VB])
                lim = 16 + 2 * t + hf
                mset("dve", vb[:, hf, lim:32], -1e30, [VB])
            for qs in range(4):
                hf = qs // 2
                qsl = slice(qs * 128, (qs + 1) * 128)
                for h in range(H):
                    mm(bG[:, h * 32:(h + 1) * 32], qT[:, h, qsl], ksT[:, h, :], True, True, [QT, KST], [BG], inc=(h == H - 1))
                tt("dve", gsb[:], bG[:, 0:256].rearrange("p (h j) -> p h j", j=32),
                   vb[:, hf:hf + 1, :].to_broadcast([128, H, 32]), ALU.add, [BG, VB], [GSB])
                for h in range(H):
                    op("dve", lambda e, h=h: e.max(out=t8[:, h, :], in_=gsb[:, h, :]), reads=[GSB], writes=[T8])
                ts("dve", thr[:], t8[:, :, 2], -1e29, None, ALU.max, None, [T8], [THR])
                tt("dve", m01[:], gsb[:], thr[:].unsqueeze(2).to_broadcast([128, H, 32]), ALU.is_lt, [GSB, THR], [M01])
                own = 16 + 2 * t + hf
                mset("dve", m01[:, :, own:own + 1], 0.0, [M01])
                btb = bTt[:, :].bitcast(BF16)
                for h in range(H):
                    tp(btb[0:32, h * 128:(h + 1) * 128], m01[:, h, :], ident[:], [M01, ID], [BTt])
                cp("act", biasT[0:32, :, qsl], btb[0:32, :].rearrange("p (h q) -> p h q", q=128), [BTt], [BIAST])

            sbanks = Rot([banks[0], banks[1], banks[6]])
            for h in range(H):
                pA, PA = banks[2 + 2 * (h % 2)]
                pB, PB_ = banks[3 + 2 * (h % 2)]
                nchunks = 8 + t
                c0 = NCT - n_ctx
                tiles = [(c, kt) for c in range(c0, nchunks + 1) for kt in range(4)]
                ntl = len(tiles)

                def emit_pv(pv, pA=pA, PA=PA, pB=pB, PB_=PB_, ntl=ntl):
                    pt, PT, lv, LV, i, sm = pv
                    mm(pA[:, :], lv, pt[:, :], i == 0, i == ntl - 1, [PT, LV], [PA], inc=True)
                    if sm is not None:
                        mm(pB[:, :], onesb[:], sm[0][:], i == 3, i == ntl - 1, [sm[1], ONES], [PB_], inc=True)
                pend = []
                aci = 0
                for i, (c, kt) in enumerate(tiles):
                    if c < nchunks and kt == 0:
                        kh, vh, KH = hrot.next()
                        kb.dma("sp", kh[:], kT_s[h, :, c * TT:(c + 1) * TT], reads=[KSK[c]], writes=[KH], sembuf=KH)
                        kb.dma("sp", vh[:], v_s[h, :, c * 4:(c + 1) * 4, :], reads=[KSV[c]], writes=[KH], sembuf=KH)
                    st_, ST_ = sbanks.next()
                    if c < nchunks:
                        lk, LK = kh[:, kt * 128:(kt + 1) * 128], KH
                        lv, LV = vh[:, kt, 0:128], KH
                        j = c * 2 + kt // 2
                    else:
                        lk, LK = kT[:, h, kt * 128:(kt + 1) * 128], KT
                        lv, LV = vaug[:, h, kt, 0:128], VAUG
                        j = 16 + 2 * t + kt // 2
                    mm(st_[:, :], lk, qT[:, h, :], True, False, [LK, QT], [ST_])
                    diag = c == nchunks
                    mm(st_[:, :], ind[:, j, :], biasT[:, h, :], False, not diag, [IND, BIAST], [ST_])
                    if diag:
                        b = kt // 2
                        mm(st_[:, b * 256:(b + 1) * 256], ident[:], causb[:, kt % 2, :], False, True, [ID, CAUS], [ST_])
                    pt, PT = ptrot.next()
                    act(pt[:], st_[:, :], AF.Exp, [ST_], [PT], scale=SCALE)
                    sm = None
                    if kt == 0:
                        prev_pt = (pt, PT)
                    elif kt == 1:
                        a0 = accs[2 * (aci % 2)]
                        tt("dve", a0[0][:], prev_pt[0][:], pt[:], ALU.add, [prev_pt[1], PT], [a0[1]])
                    elif kt == 2:
                        a1 = accs[2 * (aci % 2) + 1]
                        tt("dve", a1[0][:], a0[0][:], pt[:], ALU.add, [a0[1], PT], [a1[1]])
                    else:
                        tt("dve", a0[0][:], a1[0][:], pt[:], ALU.add, [a1[1], PT], [a0[1]])
                        sm = a0
                        aci += 1
                    if len(pend) == 2:
                        emit_pv(pend.pop(0))
                    pend.append((pt, PT, lv, LV, i, sm))
                while pend:
                    emit_pv(pend.pop(0))
                rs, RS = rsb[h % 2]
                op("dve", lambda e, rs=rs, pB=pB: e.reciprocal(out=rs[:], in_=pB[:, :]), reads=[PB_], writes=[RS])
                tt("dve", oaT[:, h, :], pA[:, :], rs[:], ALU.mult, [PA, RS], [OAT])
                run_jobs(4)

        def layernorm(sub, gi):
            row = rr[:, sub, :]
            for i in range(4):
                op("dve", lambda e, i=i: e.bn_stats(out=st[:, i * 6:(i + 1) * 6], in_=rr[:, sub, i * 512:(i + 1) * 512]),
                   reads=[RR], writes=[ST])
            op("dve", lambda e: e.bn_aggr(out=mv[:, 0:2], in_=st[:, :]), reads=[ST], writes=[MV])
            act(mv[:, 2:3], mv[:, 1:2], AF.Ln, [MV], [MV], bias=1e-5)
            act(mv[:, 3:4], mv[:, 2:3], AF.Exp, [MV], [MV], scale=-0.5)
            stt(row, row, mv[:, 0:1], lnp[:, gi * D:(gi + 1) * D], ALU.subtract, ALU.mult, [RR, MV, LNP], [RR])
            stt(row, row, mv[:, 3:4], lnp[:, (gi + 1) * D:(gi + 2) * D], ALU.mult, ALU.add, [RR, MV, LNP], [RR])

        jobs = []
        for (dst, src, rows, sem, a, b) in ((wi_s, w_in_d, D, WI, 0, 1024), (wi_s, w_in_d, D, WI, 3072, 4096), (wi_s, w_in_d, D, WI, 6144, 7168),
                                            (wi_s, w_in_d, D, WI, 7168, INW),
                                            (wpa_s, w_pa_d, 1024, WPA, None, None), (wpb_s, w_pb_d, 1024, WPB, None, None),
                                            (wo_s, w_out_d, D, WO, None, None), (wg_s, w_g_d, D, WG, None, None),
                                            (wu_s, w_u_d, D, WU, None, None), (wd_s, w_d_d, FF, WD, None, None)):
            for r0 in range(0, rows, 128):
                jobs.append((dst, src, r0, sem, a, b))

        def run_jobs(n):
            for _ in range(min(n, len(jobs))):
                dst, src, r0, sem, a, b = jobs.pop(0)
                if a is None:
                    kb.dma("pool", dst[r0:r0 + 128, :], src[r0:r0 + 128, :], writes=[sem], sembuf=sem)
                else:
                    kb.dma("pool", dst[r0:r0 + 128, a:b], src[r0:r0 + 128, a:b], writes=[sem], sembuf=sem)

        per_tile = 10 if n_ctx == NCT else (len(jobs) + n_ctx - 1) // n_ctx
        i0 = NCT - n_ctx
        alt0 = (i0 % 2 == 1)
        load_x(xcT_d, i0 * TT, alt=alt0)
        conv(wi_s, w_in_d, D, WIK, 1024, 2048)
        conv(wi_s, w_in_d, D, WIV, 2048, 3072)
        if i0 + 1 < NCT:
            load_x(xcT_d, (i0 + 1) * TT, alt=not alt0)
        conv(wi_s, w_in_d, D, WII, 5120, 6144)
        conv(wi_s, w_in_d, D, WIA, 4096, 5120)
        for i in range(i0, NCT):
            alt = (i % 2 == 1)
            XCUR[0], XCUR[1] = (xT2, XT2) if alt else (xT, XT)
            kb.activate([KT, VAUG])
            kv_proj(i)
            hgrn_ctx()
            if i + 2 < NCT:
                load_x(xcT_d, (i + 2) * TT, alt=alt)
            elif i + 2 == NCT:
                load_x(xT_d, 0)
            run_jobs(per_tile)
        if n_ctx != NCT:
            run_jobs(len(jobs))
        XCUR[0], XCUR[1] = xT, XT
        if n_ctx == 1:
            load_x(xT_d, 0)

        for t in range(n_own):
            tok0 = t * TT
            kb.activate(P1B)
            def q_cons(j, bt, bb):
                cp("act" if j % 2 else "dve", qT[:, j, :], bt[:, :], [bb], [QT])
            proj_fm(wi_s, WI, 0, 1024, 16, xrhs, [XCUR[1]], q_cons)
            run_jobs(6)
            kv_proj(8 + t)
            run_jobs(6)
            mset("dve", biasT[:], 0.0, [BIAST])
            attention(t)
            hgrn(True)
            kb.activate(P3A)
            for pr in range(8):
                wga, WGA = wtile(wview(wi_s, 0, 16, 7168 + pr * 256, 256), 16, 256, [WI])
                wgb, WGB = wtile(wview(wi_s, 0, 16, 9216 + pr * 256, 256), 16, 256, [WI])
                wpa, WPA_ = wtile(wview(wpa_s, 0, 8, pr * 256, 256), 8, 256, [WPA])
                wpb, WPB_ = wtile(wview(wpb_s, 0, 8, pr * 256, 256), 8, 256, [WPB])
                run_jobs(2)
                for j in range(2):
                    dc = pr * 2 + j
                    js = slice(j * 128, (j + 1) * 128)
                    (ta, TA), (tb2, TB2), (tc, TC), (td, TD) = p3t
                    bt, bb = bankrot.next()
                    for c in range(16):
                        mm(bt[:, :], wga[:, c, js], xT[:, c, :], c == 0, c == 15, [WGA, XT], [bb])
                    act(ta[:], bt[:, :], AF.Sigmoid, [bb], [TA])
                    bt, bb = bankrot.next()
                    for c in range(8):
                        mm(bt[:, :], wpa[:, c, js], oaT[:, c, :], c == 0, c == 7, [WPA_, OAT], [bb])
                    tt("dve", tb2[:], bt[:, :], ta[:], ALU.mult, [bb, TA], [TB2])
                    bt, bb = bankrot.next()
                    for c in range(16):
                        mm(bt[:, :], wgb[:, c, js], xT[:, c, :], c == 0, c == 15, [WGB, XT], [bb])
                    act(tc[:], bt[:, :], AF.Sigmoid, [bb], [TC])
                    bt, bb = bankrot.next()
                    for c in range(8):
                        mm(bt[:, :], wpb[:, c, js], obT[:, c, :], c == 0, c == 7, [WPB_, OBT], [bb])
                    tt("dve", td[:], bt[:, :], tc[:], ALU.mult, [bb, TC], [TD])
                    tt("dve", mrgT[:, dc, :], tb2[:], td[:], ALU.add, [TB2, TD], [MRGT])
            if t + 1 < n_own:
                load_x(xT_d, tok0 + TT)
            kb.activate(P3B)
            for sub in range(4):
                kb.dma("sp", rr[:, sub, :], x_d[tok0 + sub * 128: tok0 + (sub + 1) * 128, :], writes=[RR], sembuf=RR)
            for ct in range(4):
                def o_cons(sub, bt, bb, ct=ct):
                    stt(rr[:, sub, ct * 512:(ct + 1) * 512], rr[:, sub, ct * 512:(ct + 1) * 512], ALPHA, bt[:, :],
                        ALU.mult, ALU.add, [RR, bb], [RR])
                proj_tm(wo_s, WO, ct * 512, 16, lambda c, sub: mrgT[:, c, sub * 128:(sub + 1) * 128], [MRGT], o_cons,
                        banks[0:4] if ct % 2 == 0 else banks[4:8])
            run_jobs(len(jobs))
            trot = Rot(banks[4:8])
            for sub in range(4):
                layernorm(sub, 0)
                for g in range(4):
                    bt, bb = trot.next()
                    for i in range(4):
                        dc = g * 4 + i
                        tp(bt[:, i * 128:(i + 1) * 128], rr[:, sub, dc * 128:(dc + 1) * 128], identf[:], [RR, IDF], [bb])
                    cp("act" if g % 2 else "dve", h1T[:, g * 4:(g + 1) * 4, sub * 128:(sub + 1) * 128],
                       bt[:, :].rearrange("p (c t) -> p c t", t=128), [bb], [H1T])
            kb.activate(P3C)
            grot = Rot(banks[0:4])
            for pr in range(22):
                wgt, WGT = wtile(wview(wg_s, 0, 16, pr * 256, 256), 16, 256, [WG])
                wut, WUT = wtile(wview(wu_s, 0, 16, pr * 256, 256), 16, 256, [WU])
                for j in range(2):
                    hc = pr * 2 + j
                    js = slice(j * 128, (j + 1) * 128)
                    bg, BGb = grot.next()
                    for c in range(16):
                        mm(bg[:, :], wgt[:, c, js], h1T[:, c, :], c == 0, c == 15, [WGT, H1T], [BGb])
                    bu, BUb = grot.next()
                    for c in range(16):
                        mm(bu[:, :], wut[:, c, js], h1T[:, c, :], c == 0, c == 15, [WUT, H1T], [BUb])
                    sg_, SG_ = sgt[hc % 2]
                    act(sg_[:], bg[:, :], AF.Silu, [BGb], [SG_])
                    tt("dve", actT[:, hc, :], sg_[:], bu[:, :], ALU.mult, [SG_, BUb], [ACTT])
            for ct in range(4):
                def d_cons(sub, bt, bb, ct=ct):
                    stt(rr[:, sub, ct * 512:(ct + 1) * 512], rr[:, sub, ct * 512:(ct + 1) * 512], ALPHA, bt[:, :],
                        ALU.mult, ALU.add, [RR, bb], [RR])
                proj_tm(wd_s, WD, ct * 512, 44, lambda c, sub: actT[:, c, sub * 128:(sub + 1) * 128], [ACTT], d_cons,
                        banks[4:8] if ct % 2 == 0 else banks[0:4])
            for sub in range(4):
                layernorm(sub, 2)
                kb.dma("pool", y_d[tok0 + sub * 128: tok0 + (sub + 1) * 128, :], rr[:, sub, :], reads=[RR], writes=[YB],
                       sembuf=RRST, final=True)
        kb.emit()
    return nc


_NC_CACHE = {}
_RETURN_MAPS = False


def kernel(x, w_in, w_proj_a, w_proj_b, w_out, hgrn_norm_g, hgrn_lb_logits, ln1_g, ln1_b,
           w_gate_ffn, w_up_ffn, w_down_ffn, ln2_g, ln2_b):
    x = np.asarray(x, dtype=np.float32)
    B, T, _ = x.shape
    TOK = T // 2
    f32 = lambda a: np.ascontiguousarray(np.asarray(a, dtype=np.float32))
    shared = {
        "w_in": f32(w_in[0]), "w_pa": f32(w_proj_a[0]), "w_pb": f32(w_proj_b[0]), "w_out": f32(w_out[0]),
        "w_g": f32(w_gate_ffn[0]), "w_u": f32(w_up_ffn[0]), "w_d": f32(w_down_ffn[0]),
    }
    lbl = np.asarray(hgrn_lb_logits, dtype=np.float32).reshape(2, H, 128)
    shared["lbl"] = f32(np.concatenate([lbl[0].T, lbl[1].T], axis=1))
    shared["gn"] = f32(np.asarray(hgrn_norm_g, dtype=np.float32).reshape(H, 128).T)
    lnrow = np.concatenate([np.asarray(a, dtype=np.float32).reshape(-1) for a in (ln1_g, ln1_b, ln2_g, ln2_b)])
    shared["lnp"] = f32(np.broadcast_to(lnrow[None, :], (128, 4 * D)))
    in_maps = []
    for c in range(NCORES):
        b, s = c // 2, c % 2
        own = x[b, s * TOK:(s + 1) * TOK, :]
        m = dict(shared)
        m["x"] = f32(own)
        m["xT"] = f32(own.T)
        blk = np.zeros((128, 32), np.float32)
        if s == 0:
            m["xcT"] = np.zeros((D, TOK), np.float32)
            blk[:, 0:16] = -1e30
        else:
            m["xcT"] = f32(x[b, 0:TOK, :].T)
        m["blkb"] = blk
        in_maps.append(m)
    if _RETURN_MAPS:
        return in_maps
    if "nc" not in _NC_CACHE:
        _NC_CACHE["nc"] = build_nc()
    res = run_bass_kernel_spmd(_NC_CACHE["nc"], in_maps, core_ids=list(range(NCORES)))
    out = np.empty((B, T, D), np.float32)
    for c in range(NCORES):
        b, s = c // 2, c % 2
        out[b, s * TOK:(s + 1) * TOK, :] = res.results[c]["y"]
    return out
```

```python
import numpy as np
import concourse.bass as bass
import concourse.mybir as mybir
from concourse.bass_utils import run_bass_kernel_spmd
from contextlib import ExitStack

F32 = mybir.dt.float32
BF16 = mybir.dt.bfloat16
AF = mybir.ActivationFunctionType
ALU = mybir.AluOpType
AX = mybir.AxisListType

NCORES = 8
D = 2048
TT = 512
NT = 8
NCT = 8
H = 8
INW = 11264
FF = 5632
ALPHA = 2.0 ** 0.25
SCALE = 128.0 ** -0.5
NEG = -30000.0


class Buf:
    __slots__ = ("name", "w", "r", "dsem", "dcnt", "lo", "hi")

    def __init__(self, name):
        self.name = name
        self.w = None
        self.r = {}
        self.dsem = None
        self.dcnt = 0
        self.lo = None
        self.hi = None


class Eng:
    def __init__(self, name, sem):
        self.name = name
        self.sem = sem
        self.n = 0
        self.prog = []
        self.seen = {}


class KB:
    def __init__(self, nc, es):
        self.nc = nc
        self.es = es
        self.engs = {}
        for name in ("pe", "act", "dve", "pool", "sp"):
            sem = es.enter_context(nc.semaphore("s_" + name))
            self.engs[name] = Eng(name, sem)
        self.final = {}
        self.sbufs = []

    def buf(self, name):
        return Buf(name)

    def dmabuf(self, name):
        b = Buf(name)
        b.dsem = self.es.enter_context(self.nc.semaphore("d_" + name))
        return b

    def sb(self, name, shape, dtype, off, dma=False):
        t = self.nc.alloc_sbuf_tensor_at(name, list(shape), dtype, offset=off)
        n = 1
        for s in shape[1:]:
            n *= s
        nbytes = n * (2 if dtype == BF16 else 4)
        b = self.dmabuf(name) if dma else Buf(name)
        b.lo, b.hi = off, off + nbytes
        self.sbufs.append(b)
        return t, b

    def activate(self, bufs):
        for b in bufs:
            for o in self.sbufs:
                if o is b or o.lo is None or o.hi <= b.lo or o.lo >= b.hi:
                    continue
                evs = list(o.r.values())
                if o.w is not None:
                    evs.append(o.w)
                for ev in evs:
                    k = id(ev[0])
                    if k not in b.r or b.r[k][1] < ev[1]:
                        b.r[k] = ev

    def _waits(self, e, reads, writes):
        waits = {}

        def need(ev):
            if ev is None:
                return
            sem, val = ev
            if e.name == "pe" and sem is e.sem:
                return
            k = id(sem)
            if k not in waits or waits[k][1] < val:
                waits[k] = (sem, val)

        for b in reads:
            need(b.w)
        for b in writes:
            need(b.w)
            for ev in b.r.values():
                need(ev)
        for k, (sem, val) in waits.items():
            if e.seen.get(k, 0) < val:
                e.prog.append(("w", sem, val))
                e.seen[k] = val

    def _update(self, ev, reads, writes):
        k = id(ev[0])
        for b in reads:
            if k not in b.r or b.r[k][1] < ev[1]:
                b.r[k] = ev
        for b in writes:
            b.w = ev
            b.r = {}

    def op(self, engname, fn, reads=(), writes=(), inc=True):
        e = self.engs[engname]
        self._waits(e, reads, writes)
        if inc:
            e.n += 1
            ev = (e.sem, e.n)
            e.prog.append(("o", fn, e.sem, 1))
        else:
            ev = (e.sem, e.n + 1)
            e.prog.append(("o", fn, None, 0))
        self._update(ev, reads, writes)
        return ev

    def dma(self, qname, out, in_, reads=(), writes=(), sembuf=None, final=False):
        q = self.engs[qname]
        self._waits(q, reads, writes)
        sembuf.dcnt += 16
        ev = (sembuf.dsem, sembuf.dcnt)
        q.prog.append(("o", lambda eng, out=out, in_=in_: eng.dma_start(out=out, in_=in_), sembuf.dsem, 16))
        self._update(ev, reads, writes)
        if final:
            self.final[id(ev[0])] = ev
        return ev

    def emit(self):
        nc = self.nc
        sp = self.engs["sp"]
        for k, (sem, val) in self.final.items():
            sp.prog.append(("w", sem, val))

        def replay(e):
            def f(eng):
                for it in e.prog:
                    if it[0] == "w":
                        eng.wait_ge(it[1], it[2])
                    else:
                        ins = it[1](eng)
                        if it[2] is not None:
                            ins.then_inc(it[2], it[3])
            return f

        with nc.Block() as block:
            block.sync(replay(self.engs["sp"]))
            block.tensor(replay(self.engs["pe"]))
            block.scalar(replay(self.engs["act"]))
            block.vector(replay(self.engs["dve"]))
            block.gpsimd(replay(self.engs["pool"]))


class Rot:
    def __init__(self, items):
        self.items = items
        self.i = 0

    def next(self):
        it = self.items[self.i % len(self.items)]
        self.i += 1
        return it


def build_nc(n_own=NT, n_ctx=NCT):
    nc = bass.Bass("TRN2", target_bir_lowering=False)
    TOK = NT * TT

    def din(name, shape):
        return nc.dram_tensor(name, list(shape), F32, kind="ExternalInput").ap()

    def dscr(name, shape, dt=BF16):
        return nc.dram_tensor(name, list(shape), dt, kind="Internal").ap()

    xT_d = din("xT", [D, TOK])
    xcT_d = din("xcT", [D, TOK])
    x_d = din("x", [TOK, D])
    w_in_d = din("w_in", [D, INW])
    w_pa_d = din("w_pa", [1024, D])
    w_pb_d = din("w_pb", [1024, D])
    w_out_d = din("w_out", [D, D])
    w_g_d = din("w_g", [D, FF])
    w_u_d = din("w_u", [D, FF])
    w_d_d = din("w_d", [FF, D])
    lbl_d = din("lbl", [128, 16])
    gn_d = din("gn", [128, 8])
    lnp_d = din("lnp", [128, 4 * D])
    blkb_d = din("blkb", [128, 32])
    y_d = nc.dram_tensor("y", [TOK, D], F32, kind="ExternalOutput").ap()

    wi_s = dscr("wi_s", [D, INW])
    wpa_s = dscr("wpa_s", [1024, D])
    wpb_s = dscr("wpb_s", [1024, D])
    wo_s = dscr("wo_s", [D, D])
    wg_s = dscr("wg_s", [D, FF])
    wu_s = dscr("wu_s", [D, FF])
    wd_s = dscr("wd_s", [FF, D])
    kT_s = dscr("kT_s", [H, 128, 2 * TOK])
    v_s = dscr("v_s", [H, 128, 2 * TOK // 128, 130])

    with ExitStack() as es:
        kb = KB(nc, es)
        op = kb.op

        base = 16512
        cur = [base]

        def alloc(name, shape, dtype, dma=False):
            n = 1
            for s in shape[1:]:
                n *= s
            nbytes = n * (2 if dtype == BF16 else 4)
            nbytes = (nbytes + 31) // 32 * 32
            t, b = kb.sb(name, shape, dtype, cur[0], dma=dma)
            cur[0] += nbytes
            return t, b

        ident, ID = alloc("ident", [128, 128], BF16)
        identf, IDF = alloc("identf", [128, 128], F32)
        ind, IND = alloc("ind", [128, 32, 128], BF16)
        causb, CAUS = alloc("causb", [128, 2, 256], BF16)
        mblk, MBLK = alloc("mblk", [128, 128], F32)
        rowm, ROWM = alloc("rowm", [128, 2], F32)
        lnp, LNP = alloc("lnp", [128, 4 * D], F32, dma=True)
        blkb, BLKB = alloc("blkb", [128, 32], F32, dma=True)
        lbl, LBL = alloc("lbl", [128, 16], F32, dma=True)
        gn, GN = alloc("gn", [128, 8], F32, dma=True)
        lb, LB = alloc("lb", [128, 8], F32)
        oml, OML = alloc("oml", [128, 8], F32)
        S, SB_ = alloc("S", [128, H, 128], F32)
        ksT, KST = alloc("ksT", [128, H, 32], BF16)
        onesb, ONES = alloc("onesb", [128, 128], BF16)
        tmpi, TMPI = alloc("tmpi", [128, 256], F32)
        onesf, ONESF = alloc("onesf", [128, TT], F32)
        wslots = []
        for i in range(5):
            wslots.append(alloc("w%d" % i, [128, 4096], BF16, dma=True))
        wrot = Rot(wslots)
        xT, XT = alloc("xTt", [128, 16, TT], BF16, dma=True)
        OV0 = cur[0]
        xT2, XT2 = kb.sb("xTt2", [128, 16, TT], BF16, OV0 + 78 * 1024, dma=True)
        XCUR = [xT, XT]
        assert OV0 + 97 * 1024 + 512 <= 229344, OV0

        def ov(name, shape, dtype, koff, dma=False):
            return kb.sb(name, shape, dtype, OV0 + koff, dma=dma)

        K1 = 1024
        qT, QT = ov("qT", [128, H, TT], BF16, 32 * K1)
        kT, KT = ov("kT", [128, H, TT], BF16, 40 * K1, dma=True)
        vaug, VAUG = ov("vaug", [128, H, 4, 130], BF16, 48 * K1, dma=True)
        hsl = []
        for i in range(4):
            kh, KH = ov("kh%d" % i, [128, 512], BF16, 57 * K1 + i * 2112, dma=True)
            vh = nc.alloc_sbuf_tensor_at("vh%d" % i, [128, 4, 130], BF16, offset=OV0 + 57 * K1 + i * 2112 + 1024)
            KH.hi = OV0 + 57 * K1 + (i + 1) * 2112
            hsl.append((kh, vh, KH))
        hrot = Rot(hsl)
        pts = [ov("pt%d" % i, [128, 512], BF16, 66 * K1 + i * 1024) for i in range(4)]
        ptrot = Rot(pts)
        biasT, BIAST = ov("biasT", [128, H, TT], BF16, 70 * K1)
        oatm, OATM = ov("oatm", [128, 4, H, 128], BF16, 78 * K1)
        rsb = [ov("rs0", [128, TT], F32, 78 * K1)] * 2
        m01s = [ov("m01_%d" % i, [128, H, 32], BF16, 80 * K1 + i * 512) for i in range(4)]
        accs = [ov("acc%d" % i, [128, TT], BF16, 82 * K1 + i * 1024) for i in range(4)]
        gsb, GSB = ov("gsb", [128, H, 32], F32, 86 * K1)
        t8, T8 = ov("t8", [128, H, 8], F32, 87 * K1)
        thr, THR = ov("thr", [128, H], F32, 87 * K1 + 256)
        m01, M01 = ov("m01", [128, H, 32], BF16, 87 * K1 + 320)
        vb, VB = ov("vb", [128, 2, 32], F32, 65 * K1 + 256)
        rec, REC = ov("rec", [128, 8], F32, 65 * K1 + 512)
        oaT, OAT = ov("oaT", [128, H, TT], BF16, 88 * K1)
        obT, OBT = ov("obT", [128, H, TT], BF16, 80 * K1)
        P1B = [QT, KT, VAUG, BIAST, GSB, T8, THR, M01, VB, OAT] + [h[2] for h in hsl] + [p[1] for p in pts] + [rsb[0][1]] + [a[1] for a in accs] + [m[1] for m in m01s]
        qdT, QDT = ov("qdT", [128, H, TT], BF16, 0)
        kdT, KDT = ov("kdT", [128, H, TT], BF16, 8 * K1)
        kdcs = [ov("kdc%d" % i, [128, 2, H, 128], BF16, 16 * K1 + i * 4096) for i in range(2)]
        vv, VV = ov("vv", [128, 4, 1024], BF16, 32 * K1)
        gsig, GSIG = ov("gsig", [128, H, TT], BF16, 40 * K1)
        tmA = [ov("tmA%d" % i, [128, TT], F32, 48 * K1 + i * 2048) for i in range(5)]
        tmB = [ov("tmB%d" % i, [128, TT], F32, 24 * K1 + i * 2048) for i in range(4)] + [ov("tmB4", [128, TT], F32, 75 * K1)]
        tm = tmA
        atm, ATM = ov("atm", [128, H, 128], BF16, 60 * K1)
        smb, SMB = ov("smb", [128, H, 128], BF16, 62 * K1)
        tts, TTS = ov("tts", [128, H, 128], F32, 64 * K1)
        sq, SQ = ov("sq", [128, H, 128], BF16, 68 * K1)
        rstd, RSTD = ov("rstd", [128, H, 128], F32, 70 * K1)
        em_c, EMC = ov("em_c", [128, H, 8], F32, 74 * K1)
        el_c, ELC = ov("el_c", [128, H, 8], F32, 74 * K1 + 256)
        elb_c, ELBC = ov("elb_c", [128, H, 8], F32, 74 * K1 + 512)
        prvs = [ov("prv_c%d" % i, [128, 8], F32, 74 * K1 + 768 + i * 32) for i in range(2)]
        P2B = ([QDT, KDT, VV, GSIG, ATM, SMB, TTS, SQ, RSTD, EMC, ELC, ELBC, OBT] + [t[1] for t in tmA + tmB]
               + [k[1] for k in kdcs] + [p[1] for p in prvs])
        mrgT, MRGT = ov("mrgT", [128, 16, TT], BF16, 48 * K1)
        p3t = [ov("p3t%d" % i, [128, TT], F32, 64 * K1 + i * 2048) for i in range(4)]
        rr, RR = ov("rr", [128, 4, D], F32, 0, dma=True)
        h1T, H1T = ov("h1T", [128, 16, TT], BF16, 32 * K1)
        actT, ACTT = ov("actT", [128, 44, TT], BF16, 48 * K1)
        sgt = [ov("sgt%d" % i, [128, TT], F32, 92 * K1 + i * 2048) for i in range(2)]
        st, ST = ov("st", [128, 24], F32, 96 * K1)
        mv, MV = ov("mv", [128, 4], F32, 96 * K1 + 128)
        P3A = [MRGT] + [t[1] for t in p3t]
        P3B = [RR, H1T, ST, MV]
        P3C = [ACTT] + [t[1] for t in sgt]

        banks = []
        for i in range(8):
            t = es.enter_context(nc.psum_tensor("pb%d" % i, [128, 512], F32))
            banks.append((t, Buf("pb%d" % i)))

        WI, WPA, WPB, WO, WG, WU, WD = [kb.dmabuf(n) for n in ("cwi", "cwpa", "cwpb", "cwo", "cwg", "cwu", "cwd")]
        KSK = [Buf("ksk%d" % i) for i in range(16)]
        KSV = [Buf("ksv%d" % i) for i in range(16)]
        YB = Buf("y")
        WIA = kb.dmabuf("cwia")
        WIK = kb.dmabuf("cwik")
        WIV = kb.dmabuf("cwiv")
        WII = kb.dmabuf("cwii")
        RRST = kb.dmabuf("rrst")

        def mm(out, lhsT, rhs, start, stop, reads, writes, inc=None):
            if inc is None:
                inc = stop
            op("pe", lambda e, out=out, lhsT=lhsT, rhs=rhs, start=start, stop=stop:
               e.matmul(out, lhsT=lhsT, rhs=rhs, start=start, stop=stop, skip_group_check=True),
               reads=reads, writes=writes, inc=inc)

        def tp(out, in_, idn, reads, writes):
            op("pe", lambda e, out=out, in_=in_, idn=idn: e.transpose(out, in_, idn), reads=reads, writes=writes)

        def act(out, in_, func, reads, writes, scale=1.0, bias=0.0):
            op("act", lambda e, out=out, in_=in_, func=func, scale=scale, bias=bias:
               e.activation(out=out, in_=in_, func=func, bias=bias, scale=scale), reads=reads, writes=writes)

        def tt(eng, out, in0, in1, o, reads, writes):
            op(eng, lambda e, out=out, in0=in0, in1=in1, o=o: e.tensor_tensor(out=out, in0=in0, in1=in1, op=o),
               reads=reads, writes=writes)

        def ts(eng, out, in0, s1, s2, o0, o1, reads, writes):
            if o1 is None:
                op(eng, lambda e, out=out, in0=in0, s1=s1, o0=o0:
                   e.tensor_scalar(out=out, in0=in0, scalar1=s1, scalar2=None, op0=o0), reads=reads, writes=writes)
            else:
                op(eng, lambda e, out=out, in0=in0, s1=s1, s2=s2, o0=o0, o1=o1:
                   e.tensor_scalar(out=out, in0=in0, scalar1=s1, scalar2=s2, op0=o0, op1=o1), reads=reads, writes=writes)

        def stt(out, in0, sc, in1, o0, o1, reads, writes):
            op("dve", lambda e, out=out, in0=in0, sc=sc, in1=in1, o0=o0, o1=o1:
               e.scalar_tensor_tensor(out=out, in0=in0, scalar=sc, in1=in1, op0=o0, op1=o1), reads=reads, writes=writes)

        def cp(eng, out, in_, reads, writes):
            if eng == "act":
                act(out, in_, AF.Copy, reads, writes)
            else:
                op(eng, lambda e, out=out, in_=in_: e.tensor_copy(out=out, in_=in_), reads=reads, writes=writes)

        def mset(eng, ap, val, writes):
            op(eng, lambda e, ap=ap, val=val: e.memset(ap, val), writes=writes)

        def wtile(src_ap, kc, ncols, src_bufs):
            t, b = wrot.next()
            view = t[:, 0:kc * ncols].rearrange("p (c n) -> p c n", n=ncols)
            kb.dma("sp", view, src_ap, reads=src_bufs, writes=[b], sembuf=b)
            return view, b

        def wview(scr, r0, kc, c0, ncols):
            return scr.rearrange("(c p) n -> p c n", p=128)[:, r0:r0 + kc, c0:c0 + ncols]

        kb.dma("sp", lnp[:], lnp_d, writes=[LNP], sembuf=LNP)
        kb.dma("sp", blkb[:], blkb_d, writes=[BLKB], sembuf=BLKB)
        kb.dma("sp", lbl[:], lbl_d, writes=[LBL], sembuf=LBL)
        kb.dma("sp", gn[:], gn_d, writes=[GN], sembuf=GN)
        mset("pool", identf[:], 1.0, [IDF])
        op("pool", lambda e: e.affine_select(out=identf[:], in_=identf[:], pattern=[[-1, 128]], compare_op=ALU.is_equal,
                                             fill=0.0, base=0, channel_multiplier=1), reads=[IDF], writes=[IDF])
        cp("dve", ident[:], identf[:], [IDF], [ID])
        mset("dve", onesb[:], 1.0, [ONES])
        mset("dve", onesf[:], 1.0, [ONESF])
        mset("pool", ind[:], NEG, [IND])
        op("pool", lambda e: e.affine_select(out=ind[:], in_=ind[:], pattern=[[-1, 32], [0, 128]], compare_op=ALU.is_equal,
                                             fill=0.0, base=0, channel_multiplier=1), reads=[IND], writes=[IND])
        mset("pool", causb[:], NEG, [CAUS])
        op("pool", lambda e: e.affine_select(out=causb[:], in_=causb[:], pattern=[[128, 2], [-1, 256]], compare_op=ALU.is_gt,
                                             fill=0.0, base=0, channel_multiplier=1), reads=[CAUS], writes=[CAUS])
        mset("pool", mblk[:], 1.0, [MBLK])
        op("pool", lambda e: e.affine_select(out=mblk[:], in_=mblk[:], pattern=[[1, 128]], compare_op=ALU.is_ge,
                                             fill=0.0, base=0, channel_multiplier=-1), reads=[MBLK], writes=[MBLK])
        op("pool", lambda e: e.affine_select(out=mblk[:, 64:128], in_=mblk[:, 64:128], pattern=[[0, 64]], compare_op=ALU.is_ge,
                                             fill=0.0, base=-64, channel_multiplier=1), reads=[MBLK], writes=[MBLK])
        mset("pool", rowm[:], 1.0, [ROWM])
        op("pool", lambda e: e.affine_select(out=rowm[:, 0:1], in_=rowm[:, 0:1], pattern=[[0, 1]], compare_op=ALU.is_gt,
                                             fill=0.0, base=64, channel_multiplier=-1), reads=[ROWM], writes=[ROWM])
        op("pool", lambda e: e.affine_select(out=rowm[:, 1:2], in_=rowm[:, 1:2], pattern=[[0, 1]], compare_op=ALU.is_ge,
                                             fill=0.0, base=-64, channel_multiplier=1), reads=[ROWM], writes=[ROWM])
        tt("dve", tmpi[:, 0:8], lbl[:, 0:8], lbl[:, 8:16], ALU.subtract, [LBL], [TMPI])
        act(lb[:], tmpi[:, 0:8], AF.Sigmoid, [TMPI], [LB])
        act(oml[:], tmpi[:, 0:8], AF.Sigmoid, [TMPI], [OML], scale=-1.0)
        mset("dve", S[:], 0.0, [SB_])
        mset("dve", ksT[:], 0.0, [KST])

        def conv(dst, src, rows, sem, c0=None, c1=None):
            for r0 in range(0, rows, 128):
                if c0 is None:
                    kb.dma("pool", dst[r0:r0 + 128, :], src[r0:r0 + 128, :], writes=[sem], sembuf=sem)
                else:
                    kb.dma("pool", dst[r0:r0 + 128, c0:c1], src[r0:r0 + 128, c0:c1], writes=[sem], sembuf=sem)

        def load_x(src, tok0, alt=False):
            xt_, XT_ = (xT2, XT2) if alt else (xT, XT)
            kb.dma("pool", xt_[:], src.rearrange("(c p) t -> p c t", p=128)[:, :, tok0:tok0 + TT], writes=[XT_], sembuf=XT_)

        bankrot = Rot(banks[0:4])

        def proj_fm(scr, scr_buf, col0, ncols, kc, rhs_fn, rhs_bufs, consume, brot=None):
            br = brot or bankrot
            for pr in range(ncols // 256):
                wv, wb = wtile(wview(scr, 0, kc, col0 + pr * 256, 256), kc, 256, [scr_buf])
                for j in range(2):
                    bt, bb = br.next()
                    for c in range(kc):
                        mm(bt[:, :], wv[:, c, j * 128:(j + 1) * 128], rhs_fn(c), c == 0, c == kc - 1,
                           [wb] + rhs_bufs, [bb])
                    consume(pr * 2 + j, bt, bb)

        def proj_tm(scr, scr_buf, col0, kc, lhs_fn, lhs_bufs, consume, bset):
            pieces = []
            r = 0
            while r < kc:
                n = min(8, kc - r)
                pieces.append((r, n))
                r += n
            for (r0, n) in pieces:
                wv, wb = wtile(wview(scr, r0, n, col0, 512), n, 512, [scr_buf])
                for sub in range(4):
                    bt, bb = bset[sub]
                    for ci in range(n):
                        c = r0 + ci
                        mm(bt[:, :], lhs_fn(c, sub), wv[:, ci, :], c == 0, c == kc - 1, [wb] + lhs_bufs, [bb],
                           inc=(c == kc - 1) or (sub == 3 and ci == n - 1))
            for sub in range(4):
                consume(sub, bset[sub][0], bset[sub][1])

        xrhs = lambda c: XCUR[0][:, c, :]
        xlhs = lambda c, sub: XCUR[0][:, c, sub * 128:(sub + 1) * 128]

        def kv_proj(tile_idx, mid_hook=None):
            def k_cons(j, bt, bb):
                cp("act" if j % 2 else "dve", kT[:, j, :], bt[:, :], [bb], [KT])
            proj_fm(wi_s, WIK, 1024, 1024, 16, xrhs, [XCUR[1]], k_cons)
            op("dve", lambda e: e.tensor_reduce(out=tmpi[:, 0:16], in_=kT[:].rearrange("p h (b k) -> p (h b) k", k=256),
                                                axis=AX.X, op=ALU.add), reads=[KT], writes=[TMPI])
            cp("dve", ksT[:, :, 2 * tile_idx:2 * tile_idx + 2], tmpi[:, 0:16].rearrange("p (h b) -> p h b", b=2),
               [TMPI], [KST])
            kb.dma("pool", kT_s[:, :, tile_idx * TT:(tile_idx + 1) * TT].rearrange("h p t -> p h t"), kT[:],
                   reads=[KT], writes=[KSK[tile_idx]], sembuf=KT)
            if mid_hook is not None:
                mid_hook()
            mset("dve", vaug[:, :, :, 128:130], 1.0, [VAUG])
            for ct in range(2):
                def v_cons(sub, bt, bb, ct=ct):
                    cp("act" if sub % 2 else "dve", vaug[:, 4 * ct:4 * ct + 4, sub, 0:128],
                       bt[:, :].rearrange("p (h d) -> p h d", d=128), [bb], [VAUG])
                proj_tm(wi_s, WIV, 2048 + ct * 512, 16, xlhs, [XCUR[1]], v_cons, banks[4:8])
            kb.dma("pool", v_s[:, :, tile_idx * 4:(tile_idx + 1) * 4, :].rearrange("h p s c -> p h s c"), vaug[:],
                   reads=[VAUG], writes=[KSV[tile_idx]], sembuf=VAUG)

        def hgrn_ctx():
            kb.activate(P2B)
            for ct in range(2):
                def i_cons(sub, bt, bb, ct=ct):
                    cp("act" if sub % 2 else "dve", vv[:, sub, ct * 512:(ct + 1) * 512], bt[:, :], [bb], [VV])
                proj_tm(wi_s, WII, 5120 + ct * 512, 16, xlhs, [XCUR[1]], i_cons, banks[4:8])

            def f_cons(h, bt, bb):
                (sg, SG), (sn, SN), (gg, GG), (ep, EP), (em, EM) = tmA if h % 2 == 0 else tmB
                act(sg[:], bt[:, :], AF.Sigmoid, [bb], [SG])
                act(sn[:], bt[:, :], AF.Sigmoid, [bb], [SN], scale=-1.0)
                act(gg[:], sg[:], AF.Ln, [SG, OML, LB], [GG], scale=oml[:, h:h + 1], bias=lb[:, h:h + 1])
                op("dve", lambda e: e.tensor_tensor_scan(out=sg[:], data0=onesf[:], data1=gg[:], initial=0.0,
                                                         op0=ALU.mult, op1=ALU.add), reads=[ONESF, GG], writes=[SG])
                act(ep[:], sg[:], AF.Exp, [SG], [EP], scale=-1.0, bias=sg[:, TT - 1:TT])
                stt(kdT[:, h, :], sn[:], oml[:, h:h + 1], ep[:], ALU.mult, ALU.mult, [SN, OML, EP], [KDT])
                act(elb_c[:, h, 0:1], sg[:, TT - 1:TT], AF.Exp, [SG], [ELBC])
            proj_fm(wi_s, WIA, 4096, 1024, 16, xrhs, [XCUR[1]], f_cons)
            bU = banks[0:2]
            trb = Rot(banks[2:4])
            for sub in range(4):
                ssl = slice(sub * 128, (sub + 1) * 128)
                bT_, BT_ = trb.next()
                btb = bT_[:, :].bitcast(BF16)
                for h in range(H):
                    tp(btb[:, h * 128:(h + 1) * 128], kdT[:, h, ssl], ident[:], [KDT, ID], [BT_])
                kdc, KDC = kdcs[sub % 2]
                cp("act" if sub % 2 else "dve", kdc[:, 0, :, :], btb.rearrange("p (h k) -> p h k", k=128), [BT_], [KDC])
                for h in range(H):
                    bt, bb = bU[h // 4]
                    mm(bt[:, (h % 4) * 128:(h % 4 + 1) * 128], kdc[:, 0, h, :], vv[:, sub, h * 128:(h + 1) * 128],
                       sub == 0 and h % 4 == 0, sub == 3, [KDC, VV], [bb], inc=(h % 4 == 3))
            tt("dve", tts[:], S[:], elb_c[:, :, 0:1].to_broadcast([128, H, 128]), ALU.mult, [SB_, ELBC], [TTS])
            for g in range(2):
                bt, bb = bU[g]
                tt("dve", S[:, 4 * g:4 * g + 4, :], bt[:, :].rearrange("p (h v) -> p h v", v=128),
                   tts[:, 4 * g:4 * g + 4, :], ALU.add, [bb, TTS], [SB_])

        def hgrn(with_out):
            kb.activate(P2B)
            T_SIG = tmA[0]
            if with_out:
                def q_cons(j, bt, bb):
                    act(qdT[:, j, :], bt[:, :], AF.Silu, [bb], [QDT])
                proj_fm(wi_s, WI, 3072, 1024, 16, xrhs, [XCUR[1]], q_cons)
                run_jobs(8)
            for ct in range(2):
                def i_cons(sub, bt, bb, ct=ct):
                    cp("act" if sub % 2 else "dve", vv[:, sub, ct * 512:(ct + 1) * 512], bt[:, :], [bb], [VV])
                proj_tm(wi_s, WII, 5120 + ct * 512, 16, xlhs, [XCUR[1]], i_cons, banks[4:8])

            def f_cons(h, bt, bb):
                (sg, SG), (sn, SN), (gg, GG), (ep, EP), (em, EM) = tmA if h % 2 == 0 else tmB
                prv_c, PRVC = prvs[h % 2]
                act(sg[:], bt[:, :], AF.Sigmoid, [bb], [SG])
                act(sn[:], bt[:, :], AF.Sigmoid, [bb], [SN], scale=-1.0)
                act(gg[:], sg[:], AF.Ln, [SG, OML, LB], [GG], scale=oml[:, h:h + 1], bias=lb[:, h:h + 1])
                op("dve", lambda e: e.tensor_tensor_scan(out=sg[:], data0=onesf[:], data1=gg[:], initial=0.0,
                                                         op0=ALU.mult, op1=ALU.add), reads=[ONESF, GG], writes=[SG])
                Bc = sg[:].rearrange("p (c t) -> p c t", t=64)
                tt("dve", gg[:].rearrange("p (c t) -> p c t", t=64), Bc, Bc[:, :, 31:32].to_broadcast([128, 8, 64]),
                   ALU.subtract, [SG], [GG])
                act(ep[:], gg[:], AF.Exp, [GG], [EP])
                act(em[:], gg[:], AF.Exp, [GG], [EM], scale=-1.0)
                stt(kdT[:, h, :], sn[:], oml[:, h:h + 1], em[:], ALU.mult, ALU.mult, [SN, OML, EM], [KDT])
                if with_out:
                    tt("dve", qdT[:, h, :], qdT[:, h, :], ep[:], ALU.mult, [QDT, EP], [QDT])
                mset("dve", prv_c[:, 0:1], 0.0, [PRVC])
                cp("dve", prv_c[:, 1:8], Bc[:, 0:7, 63], [SG], [PRVC])
                tt("dve", prv_c[:, :], Bc[:, :, 31], prv_c[:, :], ALU.subtract, [SG, PRVC], [PRVC])
                act(em_c[:, h, :], prv_c[:, :], AF.Exp, [PRVC], [EMC])
                cp("dve", el_c[:, h, :], ep[:].rearrange("p (c t) -> p c t", t=64)[:, :, 63], [EP], [ELC])
                tt("dve", elb_c[:, h, :], em_c[:, h, :], el_c[:, h, :], ALU.mult, [EMC, ELC], [ELBC])
            proj_fm(wi_s, WIA, 4096, 1024, 16, xrhs, [XCUR[1]], f_cons)

            bA = banks[0:2]
            bO = banks[2:4]
            bU = banks[4:6]
            bT_, BT_ = banks[6]
            bS = banks[6:8]
            ogw = [None]

            def og_task(j):
                if j % 2 == 0:
                    ogw[0] = wtile(wview(wi_s, 0, 16, 6144 + (j // 2) * 256, 256), 16, 256, [WI])
                wv, wb = ogw[0]
                bt, bb = banks[7]
                for c in range(16):
                    mm(bt[:, :], wv[:, c, (j % 2) * 128:(j % 2 + 1) * 128], xrhs(c), c == 0, c == 15, [wb, XCUR[1]], [bb])
                tg, TG = (tmA if j % 2 == 0 else tmB)[0]
                act(tg[:], bt[:, :], AF.Sigmoid, [bb], [TG])
                ts("pool", gsig[:, j, :], tg[:], gn[:, j:j + 1], 1.0, ALU.mult, ALU.mult, [TG, GN], [GSIG])
            for sub in range(4):
                ssl = slice(sub * 128, (sub + 1) * 128)
                btb = bT_[:, :].bitcast(BF16)
                for h in range(H):
                    tp(btb[:, h * 128:(h + 1) * 128], kdT[:, h, ssl], ident[:], [KDT, ID], [BT_])
                kdc, KDC = kdcs[sub % 2]
                for c in range(2):
                    if c == 0:
                        ts("dve", kdc[:, c, :, :], btb.rearrange("p (h k) -> p h k", k=128), rowm[:, c:c + 1], None,
                           ALU.mult, None, [BT_, ROWM], [KDC])
                    else:
                        act(kdc[:, c, :, :], btb.rearrange("p (h k) -> p h k", k=128), AF.Copy, [BT_, ROWM], [KDC],
                            scale=rowm[:, c:c + 1])
                if with_out:
                    for h in range(H):
                        bt, bb = bA[h // 4]
                        mm(bt[:, (h % 4) * 128:(h % 4 + 1) * 128], kdT[:, h, ssl], qdT[:, h, ssl], True, True, [KDT, QDT], [bb])
                    for g in range(2):
                        bt, bb = bA[g]
                        tt("dve", atm[:, 4 * g:4 * g + 4, :], bt[:, :].rearrange("p (h t) -> p h t", t=128),
                           mblk[:, :].unsqueeze(1).to_broadcast([128, 4, 128]), ALU.mult, [bb, MBLK], [ATM])
                    for h in range(H):
                        bt, bb = bO[h // 4]
                        mm(bt[:, (h % 4) * 128:(h % 4 + 1) * 128], vv[:, sub, h * 128:(h + 1) * 128], atm[:, h, :],
                           h % 4 == 0, False, [VV, ATM], [bb], inc=(h == H - 1))
                for c in range(2):
                    cc = sub * 2 + c
                    if with_out:
                        tt("dve", smb[:], S[:], em_c[:, :, cc:cc + 1].to_broadcast([128, H, 128]), ALU.mult, [SB_, EMC], [SMB])
                        for h in range(H):
                            bt, bb = bO[h // 4]
                            o0 = (h % 4) * 128 + c * 64
                            mm(bt[:, o0:o0 + 64], smb[:, h, :], qdT[:, h, sub * 128 + c * 64: sub * 128 + (c + 1) * 64],
                               False, c == 1, [SMB, QDT], [bb], inc=(h == H - 1))
                    for h in range(H):
                        bt, bb = bU[h // 4]
                        mm(bt[:, (h % 4) * 128:(h % 4 + 1) * 128], kdc[:, c, h, :], vv[:, sub, h * 128:(h + 1) * 128],
                           True, True, [KDC, VV], [bb], inc=(h % 4 == 3))
                    tt("dve", tts[:], S[:], elb_c[:, :, cc:cc + 1].to_broadcast([128, H, 128]), ALU.mult, [SB_, ELBC], [TTS])
                    for g in range(2):
                        bt, bb = bU[g]
                        tt("dve", S[:, 4 * g:4 * g + 4, :], bt[:, :].rearrange("p (h v) -> p h v", v=128),
                           el_c[:, 4 * g:4 * g + 4, cc:cc + 1].to_broadcast([128, 4, 128]), ALU.mult, [bb, ELC], [SB_])
                    tt("dve", S[:], S[:], tts[:], ALU.add, [SB_, TTS], [SB_])
                    if with_out:
                        og_task(cc)
                if with_out:
                    for g in range(2):
                        bt, bb = bO[g]
                        act(sq[:, 4 * g:4 * g + 4, :], bt[:, :].rearrange("p (h t) -> p h t", t=128), AF.Square, [bb], [SQ])
                    for h in range(H):
                        bt, bb = bS[h // 4]
                        mm(bt[:, (h % 4) * 128:(h % 4 + 1) * 128], onesb[:], sq[:, h, :], True, True, [ONES, SQ], [bb],
                           inc=(h % 4 == 3))
                    for g in range(2):
                        bt, bb = bS[g]
                        act(rstd[:, 4 * g:4 * g + 4, :], bt[:, :].rearrange("p (h t) -> p h t", t=128), AF.Ln, [bb], [RSTD],
                            scale=1.0 / 128.0, bias=1e-6)
                    act(rstd[:], rstd[:], AF.Exp, [RSTD], [RSTD], scale=-0.5)
                    for g in range(2):
                        bt, bb = bO[g]
                        tt("dve", obT[:, 4 * g:4 * g + 4, ssl], bt[:, :].rearrange("p (h t) -> p h t", t=128),
                           rstd[:, 4 * g:4 * g + 4, :], ALU.mult, [bb, RSTD], [OBT])
            if with_out:
                tt("dve", obT[:], obT[:], gsig[:], ALU.mult, [OBT, GSIG], [OBT])

        def gate_front(t):
            for hf in range(2):
                cp("dve", vb[:, hf, :], blkb[:], [BLKB], [VB])
                lim = 16 + 2 * t + hf
                mset("dve", vb[:, hf, lim:32], -1e30, [VB])
            for qs in range(4):
                qsl = slice(qs * 128, (qs + 1) * 128)
                bG, BG = banks[qs // 2]
                o0 = (qs % 2) * 256
                for h in range(H):
                    mm(bG[:, o0 + h * 32:o0 + (h + 1) * 32], qT[:, h, qsl], ksT[:, h, :], True, True, [QT, KST], [BG],
                       inc=(h == H - 1))
            for qs in range(4):
                hf = qs // 2
                bG, BG = banks[qs // 2]
                o0 = (qs % 2) * 256
                m01q, M01Q = m01s[qs]
                tt("dve", gsb[:], bG[:, o0:o0 + 256].rearrange("p (h j) -> p h j", j=32),
                   vb[:, hf:hf + 1, :].to_broadcast([128, H, 32]), ALU.add, [BG, VB], [GSB])
                for h in range(H):
                    op("dve", lambda e, h=h: e.max(out=t8[:, h, :], in_=gsb[:, h, :]), reads=[GSB], writes=[T8])
                ts("dve", thr[:], t8[:, :, 2], -1e29, None, ALU.max, None, [T8], [THR])
                tt("dve", m01q[:], gsb[:], thr[:].unsqueeze(2).to_broadcast([128, H, 32]), ALU.is_lt, [GSB, THR], [M01Q])
                own = 16 + 2 * t + hf
                mset("dve", m01q[:, :, own:own + 1], 0.0, [M01Q])

        def attention(t):
            trg = Rot([banks[2], banks[3]])
            for qs in range(4):
                qsl = slice(qs * 128, (qs + 1) * 128)
                m01q, M01Q = m01s[qs]
                bTt, BTt = trg.next()
                btb = bTt[:, :].bitcast(BF16)
                for h in range(H):
                    tp(btb[0:32, h * 128:(h + 1) * 128], m01q[:, h, :], ident[:], [M01Q, ID], [BTt])
                cp("act", biasT[0:32, :, qsl], btb[0:32, :].rearrange("p (h q) -> p h q", q=128), [BTt], [BIAST])

            sbanks = Rot([banks[0], banks[1], banks[6]])
            for h in range(H):
                pA, PA = banks[2 + 2 * (h % 2)]
                pB, PB_ = banks[3 + 2 * (h % 2)]
                nchunks = 8 + t
                c0 = NCT - n_ctx
                tiles = [(c, kt) for c in range(c0, nchunks + 1) for kt in range(4)]
                ntl = len(tiles)

                def emit_pv(pv, pA=pA, PA=PA, pB=pB, PB_=PB_, ntl=ntl):
                    pt, PT, lv, LV, i, sm = pv
                    mm(pA[:, :], lv, pt[:, :], i == 0, i == ntl - 1, [PT, LV], [PA], inc=True)
                    if sm is not None:
                        mm(pB[:, :], onesb[:], sm[0][:], i == 3, i == ntl - 1, [sm[1], ONES], [PB_], inc=True)
                pend = []
                aci = 0
                for i, (c, kt) in enumerate(tiles):
                    if c < nchunks and kt == 0:
                        kh, vh, KH = hrot.next()
                        kb.dma("sp", kh[:], kT_s[h, :, c * TT:(c + 1) * TT], reads=[KSK[c]], writes=[KH], sembuf=KH)
                        kb.dma("sp", vh[:], v_s[h, :, c * 4:(c + 1) * 4, :], reads=[KSV[c]], writes=[KH], sembuf=KH)
                    st_, ST_ = sbanks.next()
                    if c < nchunks:
                        lk, LK = kh[:, kt * 128:(kt + 1) * 128], KH
                        lv, LV = vh[:, kt, 0:128], KH
                        j = c * 2 + kt // 2
                    else:
                        lk, LK = kT[:, h, kt * 128:(kt + 1) * 128], KT
                        lv, LV = vaug[:, h, kt, 0:128], VAUG
                        j = 16 + 2 * t + kt // 2
                    mm(st_[:, :], lk, qT[:, h, :], True, False, [LK, QT], [ST_])
                    diag = c == nchunks
                    mm(st_[:, :], ind[:, j, :], biasT[:, h, :], False, not diag, [IND, BIAST], [ST_])
                    if diag:
                        b = kt // 2
                        mm(st_[:, b * 256:(b + 1) * 256], ident[:], causb[:, kt % 2, :], False, True, [ID, CAUS], [ST_])
                    pt, PT = ptrot.next()
                    act(pt[:], st_[:, :], AF.Exp, [ST_], [PT], scale=SCALE)
                    sm = None
                    if kt == 0:
                        prev_pt = (pt, PT)
                    elif kt == 1:
                        a0 = accs[2 * (aci % 2)]
                        tt("dve", a0[0][:], prev_pt[0][:], pt[:], ALU.add, [prev_pt[1], PT], [a0[1]])
                    elif kt == 2:
                        a1 = accs[2 * (aci % 2) + 1]
                        tt("dve", a1[0][:], a0[0][:], pt[:], ALU.add, [a0[1], PT], [a1[1]])
                    else:
                        tt("dve", a0[0][:], a1[0][:], pt[:], ALU.add, [a1[1], PT], [a0[1]])
                        sm = a0
                        aci += 1
                    if len(pend) == 2:
                        emit_pv(pend.pop(0))
                    pend.append((pt, PT, lv, LV, i, sm))
                while pend:
                    emit_pv(pend.pop(0))
                rs, RS = rsb[h % 2]
                op("dve", lambda e, rs=rs, pB=pB: e.reciprocal(out=rs[:], in_=pB[:, :]), reads=[PB_], writes=[RS])
                tt("dve", oaT[:, h, :], pA[:, :], rs[:], ALU.mult, [PA, RS], [OAT])
                run_jobs(4)

        def layernorm(sub, gi):
            row = rr[:, sub, :]
            for i in range(4):
                op("dve", lambda e, i=i: e.bn_stats(out=st[:, i * 6:(i + 1) * 6], in_=rr[:, sub, i * 512:(i + 1) * 512]),
                   reads=[RR], writes=[ST])
            op("dve", lambda e: e.bn_aggr(out=mv[:, 0:2], in_=st[:, :]), reads=[ST], writes=[MV])
            act(mv[:, 2:3], mv[:, 1:2], AF.Ln, [MV], [MV], bias=1e-5)
            act(mv[:, 3:4], mv[:, 2:3], AF.Exp, [MV], [MV], scale=-0.5)
            stt(row, row, mv[:, 0:1], lnp[:, gi * D:(gi + 1) * D], ALU.subtract, ALU.mult, [RR, MV, LNP], [RR])
            stt(row, row, mv[:, 3:4], lnp[:, (gi + 1) * D:(gi + 2) * D], ALU.mult, ALU.add, [RR, MV, LNP], [RR])

        jobs = []
        for (dst, src, rows, sem, a, b) in ((wi_s, w_in_d, D, WI, 0, 1024), (wi_s, w_in_d, D, WI, 3072, 4096), (wi_s, w_in_d, D, WI, 6144, 7168),
                                            (wi_s, w_in_d, D, WI, 7168, INW),
                                            (wpa_s, w_pa_d, 1024, WPA, None, None), (wpb_s, w_pb_d, 1024, WPB, None, None),
                                            (wo_s, w_out_d, D, WO, None, None), (wg_s, w_g_d, D, WG, None, None),
                                            (wu_s, w_u_d, D, WU, None, None), (wd_s, w_d_d, FF, WD, None, None)):
            for r0 in range(0, rows, 128):
                jobs.append((dst, src, r0, sem, a, b))

        def run_jobs(n):
            for _ in range(min(n, len(jobs))):
                dst, src, r0, sem, a, b = jobs.pop(0)
                if a is None:
                    kb.dma("pool", dst[r0:r0 + 128, :], src[r0:r0 + 128, :], writes=[sem], sembuf=sem)
                else:
                    kb.dma("pool", dst[r0:r0 + 128, a:b], src[r0:r0 + 128, a:b], writes=[sem], sembuf=sem)

        per_tile = 10 if n_ctx == NCT else (len(jobs) + n_ctx - 1) // n_ctx
        i0 = NCT - n_ctx
        alt0 = (i0 % 2 == 1)
        load_x(xcT_d, i0 * TT, alt=alt0)
        conv(wi_s, w_in_d, D, WIK, 1024, 2048)
        conv(wi_s, w_in_d, D, WIV, 2048, 3072)
        if i0 + 1 < NCT:
            load_x(xcT_d, (i0 + 1) * TT, alt=not alt0)
        conv(wi_s, w_in_d, D, WII, 5120, 6144)
        conv(wi_s, w_in_d, D, WIA, 4096, 5120)
        for i in range(i0, NCT):
            alt = (i % 2 == 1)
            XCUR[0], XCUR[1] = (xT2, XT2) if alt else (xT, XT)
            kb.activate([KT, VAUG])
            kv_proj(i)
            hgrn_ctx()
            if i + 2 < NCT:
                load_x(xcT_d, (i + 2) * TT, alt=alt)
            elif i + 2 == NCT:
                load_x(xT_d, 0)
            run_jobs(per_tile)
        if n_ctx != NCT:
            run_jobs(len(jobs))
        XCUR[0], XCUR[1] = xT, XT
        if n_ctx == 1:
            load_x(xT_d, 0)

        for t in range(n_own):
            tok0 = t * TT
            kb.activate(P1B)
            def q_cons(j, bt, bb):
                cp("act" if j % 2 else "dve", qT[:, j, :], bt[:, :], [bb], [QT])
            proj_fm(wi_s, WI, 0, 1024, 16, xrhs, [XCUR[1]], q_cons)
            run_jobs(6)
            mset("dve", biasT[:], 0.0, [BIAST])
            kv_proj(8 + t, mid_hook=lambda t=t: gate_front(t))
            run_jobs(6)
            attention(t)
            hgrn(True)
            kb.activate(P3A)
            for pr in range(8):
                wga, WGA = wtile(wview(wi_s, 0, 16, 7168 + pr * 256, 256), 16, 256, [WI])
                wgb, WGB = wtile(wview(wi_s, 0, 16, 9216 + pr * 256, 256), 16, 256, [WI])
                wpa, WPA_ = wtile(wview(wpa_s, 0, 8, pr * 256, 256), 8, 256, [WPA])
                wpb, WPB_ = wtile(wview(wpb_s, 0, 8, pr * 256, 256), 8, 256, [WPB])
                run_jobs(2)
                for j in range(2):
                    dc = pr * 2 + j
                    js = slice(j * 128, (j + 1) * 128)
                    (ta, TA), (tb2, TB2), (tc, TC), (td, TD) = p3t
                    bt, bb = bankrot.next()
                    for c in range(16):
                        mm(bt[:, :], wga[:, c, js], xT[:, c, :], c == 0, c == 15, [WGA, XT], [bb])
                    act(ta[:], bt[:, :], AF.Sigmoid, [bb], [TA])
                    bt, bb = bankrot.next()
                    for c in range(8):
                        mm(bt[:, :], wpa[:, c, js], oaT[:, c, :], c == 0, c == 7, [WPA_, OAT], [bb])
                    tt("dve", tb2[:], bt[:, :], ta[:], ALU.mult, [bb, TA], [TB2])
                    bt, bb = bankrot.next()
                    for c in range(16):
                        mm(bt[:, :], wgb[:, c, js], xT[:, c, :], c == 0, c == 15, [WGB, XT], [bb])
                    act(tc[:], bt[:, :], AF.Sigmoid, [bb], [TC])
                    bt, bb = bankrot.next()
                    for c in range(8):
                        mm(bt[:, :], wpb[:, c, js], obT[:, c, :], c == 0, c == 7, [WPB_, OBT], [bb])
                    tt("dve", td[:], bt[:, :], tc[:], ALU.mult, [bb, TC], [TD])
                    tt("dve", mrgT[:, dc, :], tb2[:], td[:], ALU.add, [TB2, TD], [MRGT])
            if t + 1 < n_own:
                load_x(xT_d, tok0 + TT)
            kb.activate(P3B)
            for sub in range(4):
                kb.dma("sp", rr[:, sub, :], x_d[tok0 + sub * 128: tok0 + (sub + 1) * 128, :], writes=[RR], sembuf=RR)
            for ct in range(4):
                def o_cons(sub, bt, bb, ct=ct):
                    stt(rr[:, sub, ct * 512:(ct + 1) * 512], rr[:, sub, ct * 512:(ct + 1) * 512], ALPHA, bt[:, :],
                        ALU.mult, ALU.add, [RR, bb], [RR])
                proj_tm(wo_s, WO, ct * 512, 16, lambda c, sub: mrgT[:, c, sub * 128:(sub + 1) * 128], [MRGT], o_cons,
                        banks[0:4] if ct % 2 == 0 else banks[4:8])
            run_jobs(len(jobs))
            trot = Rot(banks[4:8])
            for sub in range(4):
                layernorm(sub, 0)
                for g in range(4):
                    bt, bb = trot.next()
                    for i in range(4):
                        dc = g * 4 + i
                        tp(bt[:, i * 128:(i + 1) * 128], rr[:, sub, dc * 128:(dc + 1) * 128], identf[:], [RR, IDF], [bb])
                    cp("act" if g % 2 else "dve", h1T[:, g * 4:(g + 1) * 4, sub * 128:(sub + 1) * 128],
                       bt[:, :].rearrange("p (c t) -> p c t", t=128), [bb], [H1T])
            kb.activate(P3C)
            grot = Rot(banks[0:4])
            for pr in range(22):
                wgt, WGT = wtile(wview(wg_s, 0, 16, pr * 256, 256), 16, 256, [WG])
                wut, WUT = wtile(wview(wu_s, 0, 16, pr * 256, 256), 16, 256, [WU])
                for j in range(2):
                    hc = pr * 2 + j
                    js = slice(j * 128, (j + 1) * 128)
                    bg, BGb = grot.next()
                    for c in range(16):
                        mm(bg[:, :], wgt[:, c, js], h1T[:, c, :], c == 0, c == 15, [WGT, H1T], [BGb])
                    bu, BUb = grot.next()
                    for c in range(16):
                        mm(bu[:, :], wut[:, c, js], h1T[:, c, :], c == 0, c == 15, [WUT, H1T], [BUb])
                    sg_, SG_ = sgt[hc % 2]
                    act(sg_[:], bg[:, :], AF.Silu, [BGb], [SG_])
                    tt("dve", actT[:, hc, :], sg_[:], bu[:, :], ALU.mult, [SG_, BUb], [ACTT])
            for ct in range(4):
                def d_cons(sub, bt, bb, ct=ct):
                    stt(rr[:, sub, ct * 512:(ct + 1) * 512], rr[:, sub, ct * 512:(ct + 1) * 512], ALPHA, bt[:, :],
                        ALU.mult, ALU.add, [RR, bb], [RR])
                proj_tm(wd_s, WD, ct * 512, 44, lambda c, sub: actT[:, c, sub * 128:(sub + 1) * 128], [ACTT], d_cons,
                        banks[4:8] if ct % 2 == 0 else banks[0:4])
            for sub in range(4):
                layernorm(sub, 2)
                kb.dma("pool", y_d[tok0 + sub * 128: tok0 + (sub + 1) * 128, :], rr[:, sub, :], reads=[RR], writes=[YB],
                       sembuf=RRST, final=True)
        kb.emit()
    return nc


_NC_CACHE = {}
_RETURN_MAPS = False


def kernel(x, w_in, w_proj_a, w_proj_b, w_out, hgrn_norm_g, hgrn_lb_logits, ln1_g, ln1_b,
           w_gate_ffn, w_up_ffn, w_down_ffn, ln2_g, ln2_b):
    x = np.asarray(x, dtype=np.float32)
    B, T, _ = x.shape
    TOK = T // 2
    f32 = lambda a: np.ascontiguousarray(np.asarray(a, dtype=np.float32))
    shared = {
        "w_in": f32(w_in[0]), "w_pa": f32(w_proj_a[0]), "w_pb": f32(w_proj_b[0]), "w_out": f32(w_out[0]),
        "w_g": f32(w_gate_ffn[0]), "w_u": f32(w_up_ffn[0]), "w_d": f32(w_down_ffn[0]),
    }
    lbl = np.asarray(hgrn_lb_logits, dtype=np.float32).reshape(2, H, 128)
    shared["lbl"] = f32(np.concatenate([lbl[0].T, lbl[1].T], axis=1))
    shared["gn"] = f32(np.asarray(hgrn_norm_g, dtype=np.float32).reshape(H, 128).T)
    lnrow = np.concatenate([np.asarray(a, dtype=np.float32).reshape(-1) for a in (ln1_g, ln1_b, ln2_g, ln2_b)])
    shared["lnp"] = f32(np.broadcast_to(lnrow[None, :], (128, 4 * D)))
    in_maps = []
    for c in range(NCORES):
        b, s = c // 2, c % 2
        own = x[b, s * TOK:(s + 1) * TOK, :]
        m = dict(shared)
        m["x"] = f32(own)
        m["xT"] = f32(own.T)
        blk = np.zeros((128, 32), np.float32)
        if s == 0:
            m["xcT"] = np.zeros((D, TOK), np.float32)
            blk[:, 0:16] = -1e30
        else:
            m["xcT"] = f32(x[b, 0:TOK, :].T)
        m["blkb"] = blk
        in_maps.append(m)
    if _RETURN_MAPS:
        return in_maps
    if "nc" not in _NC_CACHE:
        _NC_CACHE["nc"] = build_nc()
    res = run_bass_kernel_spmd(_NC_CACHE["nc"], in_maps, core_ids=list(range(NCORES)))
    out = np.empty((B, T, D), np.float32)
    for c in range(NCORES):
        b, s = c // 2, c % 2
        out[b, s * TOK:(s + 1) * TOK, :] = res.results[c]["y"]
    return out
```
